# Optimizing a Trainium2 kernel written in Bass

```python
import jax, jax.numpy as jnp
from jax import lax
import numpy as np

D_MODEL = 1024
BATCH = 8
SEQ = 2048
DEPTH = 4
DEC_BATCH = 128
DEC_SEQ = 8
PAST_LEN = 16384
PAGE_SIZE = 128

POOL_W = D_MODEL
POOL_GROUPS = 4
POOL_GW = POOL_W // POOL_GROUPS
POOL_WINDOWS = (2, 4, 8, 16)
POOL_PREFIX = 15
LRU_W = D_MODEL
LRU_HEADS = 8
LRU_HD = LRU_W // LRU_HEADS
CONV_W = 4
LRU_C = 8.0
GMLP_W = D_MODEL // 2
GMLP_GROUPS = 4
GMLP_GW = GMLP_W // GMLP_GROUPS
CHUNK = 128
N_BRANCH = 3
IN_W = POOL_W + 2 * LRU_W + 2 * GMLP_W + N_BRANCH * D_MODEL
D_FF = 2816
N_MOD = 9
EPS = 1e-6

kernel_name = "hybrid_pool_lru_gmlp_macaron_decoder_step"


def rmsnorm(x, g):
    xf = x.astype(jnp.float32)
    y = xf * lax.rsqrt(jnp.mean(xf * xf, axis=-1, keepdims=True) + EPS)
    return (y * g.astype(jnp.float32)).astype(x.dtype)


def swiglu(h, wg, wu, wd):
    return (jax.nn.silu(h @ wg) * (h @ wu)) @ wd


def pool_mixer(xp, prefix, start_pos, w_grp, scale):
    B, T, _ = xp.shape
    xcat = jnp.concatenate([prefix.astype(xp.dtype), xp], axis=1)
    cs = jnp.cumsum(xcat.astype(jnp.float32), axis=1)
    cs = jnp.concatenate([jnp.zeros((B, 1, POOL_W), jnp.float32), cs], axis=1)
    pos = start_pos + jnp.arange(T, dtype=jnp.int32)
    hi = cs[:, POOL_PREFIX + 1:POOL_PREFIX + 1 + T]
    means = []
    for g, w in enumerate(POOL_WINDOWS):
        c0 = g * POOL_GW
        lo = cs[:, POOL_PREFIX + 1 - w:POOL_PREFIX + 1 - w + T, c0:c0 + POOL_GW]
        cnt = jnp.minimum(w, pos + 1).astype(jnp.float32)[None, :, None]
        means.append((hi[..., c0:c0 + POOL_GW] - lo) / cnt)
    mean = jnp.stack(means, axis=2)
    d = mean - xp.astype(jnp.float32).reshape(B, T, POOL_GROUPS, POOL_GW)
    out = jnp.einsum('btgc,gcd->btgd', d.astype(xp.dtype), w_grp).reshape(B, T, POOL_W)
    return out * scale, xcat[:, -POOL_PREFIX:]


def causal_conv(xb, prefix, w, b):
    T = xb.shape[1]
    xcat = jnp.concatenate([prefix.astype(xb.dtype), xb], axis=1)
    out = b + sum(xcat[:, k:k + T] * w[k] for k in range(CONV_W))
    return out, xcat[:, -(CONV_W - 1):]


def rg_lru(xc, h0, wr, br, wi, bi, lam):
    B, T, _ = xc.shape
    xh = xc.reshape(B, T, LRU_HEADS, LRU_HD)
    r = jax.nn.sigmoid(jnp.einsum('bthc,hcd->bthd', xh, wr).reshape(B, T, LRU_W) + br)
    i = jax.nn.sigmoid(jnp.einsum('bthc,hcd->bthd', xh, wi).reshape(B, T, LRU_W) + bi)
    log_a = -LRU_C * r.astype(jnp.float32) * jax.nn.softplus(-lam.astype(jnp.float32))
    a = jnp.exp(log_a)
    bx = jnp.sqrt(-jnp.expm1(2.0 * log_a)) * (i * xc).astype(jnp.float32)

    def step(h, ab):
        a_t, b_t = ab
        h = a_t * h + b_t
        return h, h

    hT, hs = lax.scan(step, h0.astype(jnp.float32), (jnp.swapaxes(a, 0, 1), jnp.swapaxes(bx, 0, 1)))
    return jnp.swapaxes(hs, 0, 1).astype(xc.dtype), hT.astype(xc.dtype)


def chunk_gmlp(u, v, g_norm, ws, bs):
    B, T, _ = u.shape
    vn = rmsnorm(v, g_norm)
    Tp = -(-T // CHUNK) * CHUNK
    vpad = jnp.pad(vn, ((0, 0), (0, Tp - T), (0, 0)))
    vb = vpad.reshape(B, Tp // CHUNK, CHUNK, GMLP_GROUPS, GMLP_GW)
    mask = jnp.tril(jnp.ones((CHUNK, CHUNK), dtype=bool))
    wm = jnp.where(mask[None], ws, 0)
    s = jnp.einsum('gts,bnsgc->bntgc', wm, vb) + jnp.transpose(bs)[None, None, :, :, None]
    s = s.reshape(B, Tp, GMLP_W)[:, :T]
    return u * s, vn


def token_mixer(h, pre_pool, pre_conv, h0, start_pos, w_in, pool_w, pool_scale, conv_w, conv_b,
                wr, br, wi, bi, lam, gm_norm, gm_ws, gm_bs, wbo_pool, wbo_lru, wbo_gm, w_out):
    B, T, _ = h.shape
    z = h @ w_in
    idx = np.cumsum([POOL_W, LRU_W, LRU_W, GMLP_W, GMLP_W]).tolist()
    xp, xl, gl, u, v, gates = jnp.split(z, idx, axis=-1)
    y_pool, new_pool = pool_mixer(xp, pre_pool, start_pos, pool_w, pool_scale)
    xc, new_conv = causal_conv(xl, pre_conv, conv_w, conv_b)
    y_lru, hT = rg_lru(xc, h0, wr, br, wi, bi, lam)
    y_lru = y_lru * jax.nn.gelu(gl)
    y_gm, vn = chunk_gmlp(u, v, gm_norm, gm_ws, gm_bs)
    g = jax.nn.sigmoid(gates).reshape(B, T, N_BRANCH, D_MODEL)
    merged = (g[:, :, 0] * (y_pool @ wbo_pool) + g[:, :, 1] * (y_lru @ wbo_lru)
              + g[:, :, 2] * (y_gm @ wbo_gm))
    return merged @ w_out, new_pool, new_conv, hT, vn


def layer(x, c, pre_pool, pre_conv, h0, start_pos, lw):
    (w_ada, b_ada, n1, f1g, f1u, f1d, nm, w_in, pool_w, pool_scale, conv_w, conv_b,
     wr, br, wi, bi, lam, gm_norm, gm_ws, gm_bs, wbo_pool, wbo_lru, wbo_gm, w_out,
     n2, f2g, f2u, f2d) = lw
    mod = jax.nn.silu(c) @ w_ada + b_ada
    sh1, sc1, gt1, sh2, sc2, gt2, sh3, sc3, gt3 = [m[:, None, :] for m in jnp.split(mod, N_MOD, axis=-1)]
    h = rmsnorm(x, n1) * (1 + sc1) + sh1
    x = x + 0.5 * gt1 * swiglu(h, f1g, f1u, f1d)
    h = rmsnorm(x, nm) * (1 + sc2) + sh2
    y, new_pool, new_conv, hT, vn = token_mixer(h, pre_pool, pre_conv, h0, start_pos, w_in, pool_w, pool_scale,
                                                conv_w, conv_b, wr, br, wi, bi, lam, gm_norm, gm_ws, gm_bs,
                                                wbo_pool, wbo_lru, wbo_gm, w_out)
    x = x + gt2 * y
    h = rmsnorm(x, n2) * (1 + sc3) + sh3
    x = x + 0.5 * gt3 * swiglu(h, f2g, f2u, f2d)
    return x, new_pool, new_conv, hT, vn


def setup_inputs(seed: int = 0) -> dict:
    key = jax.random.key(seed)
    ks = jax.random.split(key, 48)
    f32 = jnp.float32
    nrm = lambda k, s, sc: jax.random.normal(k, s, f32) * sc
    D = D_MODEL
    a8 = jax.random.uniform(ks[20], (DEPTH, LRU_W), f32, 0.9, 0.999)
    a_base = a8 ** (1.0 / LRU_C)
    lam = jnp.log(a_base) - jnp.log1p(-a_base)
    return {
        "x_prompt": nrm(ks[0], (BATCH, SEQ, D), 1.0),
        "x_sample": nrm(ks[1], (DEC_BATCH, DEC_SEQ, D), 1.0),
        "c_prompt": nrm(ks[2], (BATCH, D), 1.0),
        "c_sample": nrm(ks[3], (DEC_BATCH, D), 1.0),
        "state_pool": nrm(ks[4], (DEPTH, DEC_BATCH, POOL_PREFIX, POOL_W), 1.0),
        "state_conv": nrm(ks[5], (DEPTH, DEC_BATCH, CONV_W - 1, LRU_W), 1.0),
        "state_lru": nrm(ks[6], (DEPTH, DEC_BATCH, LRU_W), 0.5),
        "w_ada": nrm(ks[7], (DEPTH, D, N_MOD * D), 0.5 * D ** -0.5),
        "b_ada": nrm(ks[8], (DEPTH, N_MOD * D), 0.01),
        "norm_ffn1": 1.0 + nrm(ks[9], (DEPTH, D), 0.01),
        "ffn1_w_gate": nrm(ks[10], (DEPTH, D, D_FF), D ** -0.5),
        "ffn1_w_up": nrm(ks[11], (DEPTH, D, D_FF), D ** -0.5),
        "ffn1_w_down": nrm(ks[12], (DEPTH, D_FF, D), D_FF ** -0.5),
        "norm_mix": 1.0 + nrm(ks[13], (DEPTH, D), 0.01),
        "w_in": nrm(ks[14], (DEPTH, D, IN_W), D ** -0.5),
        "pool_w": nrm(ks[15], (DEPTH, POOL_GROUPS, POOL_GW, POOL_GW), POOL_GW ** -0.5),
        "pool_scale": 1.0 + nrm(ks[16], (DEPTH, POOL_W), 0.1),
        "conv_w": nrm(ks[17], (DEPTH, CONV_W, LRU_W), CONV_W ** -0.5),
        "conv_b": nrm(ks[18], (DEPTH, LRU_W), 0.01),
        "lru_wr": nrm(ks[19], (DEPTH, LRU_HEADS, LRU_HD, LRU_HD), LRU_HD ** -0.5),
        "lru_br": nrm(ks[21], (DEPTH, LRU_W), 0.01),
        "lru_wi": nrm(ks[22], (DEPTH, LRU_HEADS, LRU_HD, LRU_HD), LRU_HD ** -0.5),
        "lru_bi": nrm(ks[23], (DEPTH, LRU_W), 0.01),
        "lru_lambda": lam,
        "gmlp_norm": 1.0 + nrm(ks[24], (DEPTH, GMLP_W), 0.01),
        "gmlp_ws": nrm(ks[25], (DEPTH, GMLP_GROUPS, CHUNK, CHUNK), 0.5 * CHUNK ** -0.5),
        "gmlp_bs": 1.0 + nrm(ks[26], (DEPTH, GMLP_GROUPS, CHUNK), 0.01),
        "wbo_pool": nrm(ks[27], (DEPTH, POOL_W, D), POOL_W ** -0.5),
        "wbo_lru": nrm(ks[28], (DEPTH, LRU_W, D), LRU_W ** -0.5),
        "wbo_gmlp": nrm(ks[29], (DEPTH, GMLP_W, D), GMLP_W ** -0.5),
        "w_out": nrm(ks[30], (DEPTH, D, D), D ** -0.5),
        "norm_ffn2": 1.0 + nrm(ks[31], (DEPTH, D), 0.01),
        "ffn2_w_gate": nrm(ks[32], (DEPTH, D, D_FF), D ** -0.5),
        "ffn2_w_up": nrm(ks[33], (DEPTH, D, D_FF), D ** -0.5),
        "ffn2_w_down": nrm(ks[34], (DEPTH, D_FF, D), D_FF ** -0.5),
        "norm_final": 1.0 + nrm(ks[35], (D,), 0.01),
    }


def reference(x_prompt, x_sample, c_prompt, c_sample, state_pool, state_conv, state_lru,
              w_ada, b_ada, norm_ffn1, ffn1_w_gate, ffn1_w_up, ffn1_w_down, norm_mix, w_in,
              pool_w, pool_scale, conv_w, conv_b, lru_wr, lru_br, lru_wi, lru_bi, lru_lambda,
              gmlp_norm, gmlp_ws, gmlp_bs, wbo_pool, wbo_lru, wbo_gmlp, w_out,
              norm_ffn2, ffn2_w_gate, ffn2_w_up, ffn2_w_down, norm_final):
    xp = x_prompt
    xs = x_sample
    zp_pool = jnp.zeros((BATCH, POOL_PREFIX, POOL_W), x_prompt.dtype)
    zp_conv = jnp.zeros((BATCH, CONV_W - 1, LRU_W), x_prompt.dtype)
    zp_h = jnp.zeros((BATCH, LRU_W), x_prompt.dtype)
    pp, pc, ph, sp, sc, sh, sv = [], [], [], [], [], [], []
    for l in range(DEPTH):
        lw = (w_ada[l], b_ada[l], norm_ffn1[l], ffn1_w_gate[l], ffn1_w_up[l], ffn1_w_down[l],
              norm_mix[l], w_in[l], pool_w[l], pool_scale[l], conv_w[l], conv_b[l],
              lru_wr[l], lru_br[l], lru_wi[l], lru_bi[l], lru_lambda[l],
              gmlp_norm[l], gmlp_ws[l], gmlp_bs[l], wbo_pool[l], wbo_lru[l], wbo_gmlp[l], w_out[l],
              norm_ffn2[l], ffn2_w_gate[l], ffn2_w_up[l], ffn2_w_down[l])
        xp, npool, nconv, nh, _ = layer(xp, c_prompt, zp_pool, zp_conv, zp_h, 0, lw)
        pp.append(npool); pc.append(nconv); ph.append(nh)
        xs, npool, nconv, nh, nv = layer(xs, c_sample, state_pool[l], state_conv[l], state_lru[l], PAST_LEN, lw)
        sp.append(npool); sc.append(nconv); sh.append(nh); sv.append(nv)
    y_prompt = rmsnorm(xp, norm_final)
    y_sample = rmsnorm(xs, norm_final)
    return (y_prompt, y_sample, jnp.stack(pp), jnp.stack(pc), jnp.stack(ph),
            jnp.stack(sp), jnp.stack(sc), jnp.stack(sh), jnp.stack(sv))
```

```python
import numpy as np
from contextlib import ExitStack
import concourse.bass as bass
import concourse.mybir as mybir
from concourse.bass_utils import run_bass_kernel_spmd

F32 = mybir.dt.float32
BF16 = mybir.dt.bfloat16
AF = mybir.ActivationFunctionType
ALU = mybir.AluOpType

NCORES = 8
P = 128
D = 1024
KC = 8
DFF = 2816
NL = 4
TPROMPT = 2048
NSEQ = 16
TSAMP = 8
EPS = 1e-6
PAGE = 512
NSLOT = 4
SLOT_EL = 4096
TMAX = 1152
HALF_FF = 11

O_XP, O_XL, O_GL, O_U, O_V, O_G0 = 0, 1024, 2048, 3072, 3584, 4096

VR = {"b_ada": 0, "n1": 72, "nm": 80, "n2": 88, "pscale": 96, "convw": 104, "convb": 136,
      "br": 144, "bi": 152, "lam": 160}
VPL = 168
V_NF = NL * VPL
V_ROWS = V_NF + 8
V_BLK = (V_ROWS + 127) // 128


class TT_:
    def __init__(self, kind, col, n, pos0):
        self.kind, self.col, self.n, self.pos0 = kind, col, n, pos0
        self.nseq = 1 if kind == "p" else NSEQ
        self.tl = n if kind == "p" else TSAMP


GROUPS = [
    [TT_("p", 0, 512, 0), TT_("p", 512, 512, 512)],
    [TT_("p", 0, 512, 1024), TT_("p", 512, 512, 1536), TT_("s", 1024, 128, 0)],
]


def _esize(dt):
    return mybir.dt.size(dt)


class Op:
    __slots__ = ("eng", "fn", "dma", "deps", "signal", "count", "sem", "target", "prev")

    def __init__(self, eng, fn, dma):
        self.eng, self.fn, self.dma = eng, fn, dma
        self.deps = ()
        self.signal = False
        self.count = 0
        self.sem = None
        self.target = 0
        self.prev = 0


class Sched:
    ENG = ("pe", "act", "dve", "pool", "sp")

    def __init__(self):
        self.ops = {e: [] for e in self.ENG}
        self.lw = {}
        self.rd = {}
        self.outs = []

    @staticmethod
    def keys(ap):
        if isinstance(ap, tuple):
            return [ap]
        if str(ap.space) == "DRAM":
            return None
        dims = list(ap.ap)
        es = _esize(ap.dtype)
        pstride = dims[0][0]
        name = ap.tensor.name
        base = (ap.offset % pstride) * es if pstride > 0 else ap.offset * es
        free = dims[1:]
        if not free:
            return [(name, base // PAGE)]
        outer = free[:-1]
        ls, lc = free[-1]
        run = (lc - 1) * abs(ls) * es + es
        starts = [base]
        for (s, c) in outer:
            if s == 0 or c == 1:
                continue
            starts = [b + i * s * es for b in starts for i in range(c)]
        pages = set()
        for st in starts:
            for pg in range(st // PAGE, (st + run - 1) // PAGE + 1):
                pages.add(pg)
        return [(name, pg) for pg in pages]

    def add(self, eng, fn, reads=(), writes=(), dma=False):
        o = Op(eng, fn, dma)
        deps = set()
        rkeys, wkeys = [], []
        for r in reads:
            if r is None or isinstance(r, (int, float)):
                continue
            k = self.keys(r)
            if k:
                rkeys.extend(k)
        isout = False
        for w in writes:
            k = self.keys(w)
            if k is None:
                isout = True
            else:
                wkeys.extend(k)
        for k in rkeys:
            w = self.lw.get(k)
            if w is not None:
                deps.add(w)
        for k in wkeys:
            w = self.lw.get(k)
            if w is not None:
                deps.add(w)
            for r in self.rd.get(k, ()):
                deps.add(r)
        for k in rkeys:
            self.rd.setdefault(k, []).append(o)
        for k in wkeys:
            self.lw[k] = o
            self.rd[k] = []
        dl = []
        for d in deps:
            if d is o:
                continue
            if d.eng == "pe" and eng == "pe" and not d.dma and not dma:
                continue
            d.signal = True
            dl.append(d)
        o.deps = dl
        if isout:
            self.outs.append(o)
        self.ops[eng].append(o)
        return o

    def finalize(self, nc, stack):
        self.esem = {}
        for e in ("pe", "act", "dve", "pool"):
            self.esem[e] = stack.enter_context(nc.semaphore("es_" + e))
        self.dsem = {"sp": [stack.enter_context(nc.semaphore("dsp%d" % i)) for i in range(40)],
                     "pool": [stack.enter_context(nc.semaphore("dpl%d" % i)) for i in range(24)],
                     "act": [stack.enter_context(nc.semaphore("dac%d" % i)) for i in range(4)]}
        fin = Op("sp", lambda e: e.nop(), False)
        fin.deps = list(self.outs)
        self.ops["sp"].append(fin)
        for e in self.ENG:
            cnt = 0
            rr = 0
            tgt = {}
            for o in self.ops[e]:
                if o.dma:
                    pool = self.dsem[e]
                    o.sem = pool[rr % len(pool)]
                    rr += 1
                    o.prev = tgt.get(id(o.sem), 0)
                    o.target = o.prev + 16
                    tgt[id(o.sem)] = o.target
                elif o.signal:
                    cnt += 1
                    o.count = cnt

    def emit(self, eng, e):
        known = {}

        def wait(sem, val):
            k = id(sem)
            if known.get(k, 0) >= val:
                return
            known[k] = val
            e.wait_ge(sem, val)

        for o in self.ops[eng]:
            need = {}
            for d in o.deps:
                if d.dma:
                    s, v = d.sem, d.target
                else:
                    s, v = self.esem[d.eng], d.count
                k = id(s)
                if k not in need or need[k][1] < v:
                    need[k] = (s, v)
            for (s, v) in need.values():
                wait(s, v)
            if o.dma and o.prev > 0:
                wait(o.sem, o.prev)
            ins = o.fn(e)
            if o.dma:
                ins.then_inc(o.sem, 16)
            elif o.signal:
                ins.then_inc(self.esem[eng], 1)


def build_nc():
    nc = bass.Bass("TRN2", target_bir_lowering=False)
    S = Sched()

    def din(name, shape):
        return nc.dram_tensor(name, list(shape), F32, kind="ExternalInput").ap()

    def dout(name, shape):
        return nc.dram_tensor(name, list(shape), F32, kind="ExternalOutput").ap()

    xp_d = din("xp", [TPROMPT, D])
    xs_d = din("xs", [NSEQ * TSAMP, D])
    c17_d = din("c17", [17, D])
    spool_d = din("spool", [NL, NSEQ * 15, D])
    sconv_d = din("sconv", [NL, NSEQ * 3, D])
    slru_d = din("slru", [NL, NSEQ, D])
    w_ada = din("w_ada", [NL, D, 9 * D])
    b_ada = din("b_ada", [NL, 9 * D])
    norm_ffn1 = din("norm_ffn1", [NL, D])
    f1g = din("ffn1_w_gate", [NL, D, DFF])
    f1u = din("ffn1_w_up", [NL, D, DFF])
    f1d = din("ffn1_w_down", [NL, DFF, D])
    norm_mix = din("norm_mix", [NL, D])
    w_in = din("w_in", [NL, D, 7168])
    pool_w = din("pool_w", [NL, 4, 256, 256])
    pool_scale = din("pool_scale", [NL, D])
    conv_w = din("conv_w", [NL, 4, D])
    conv_b = din("conv_b", [NL, D])
    lru_wr = din("lru_wr", [NL, 8, 128, 128])
    lru_br = din("lru_br", [NL, D])
    lru_wi = din("lru_wi", [NL, 8, 128, 128])
    lru_bi = din("lru_bi", [NL, D])
    lru_lambda = din("lru_lambda", [NL, D])
    gmlp_norm = din("gmlp_norm", [NL, 512])
    gmlp_ws = din("gmlp_ws", [NL, 4, 128, 128])
    gmlp_bs = din("gmlp_bs", [NL, 4, 128])
    wbo_pool = din("wbo_pool", [NL, D, D])
    wbo_lru = din("wbo_lru", [NL, D, D])
    wbo_gmlp = din("wbo_gmlp", [NL, 512, D])
    w_out = din("w_out", [NL, D, D])
    norm_ffn2 = din("norm_ffn2", [NL, D])
    f2g = din("ffn2_w_gate", [NL, D, DFF])
    f2u = din("ffn2_w_up", [NL, D, DFF])
    f2d = din("ffn2_w_down", [NL, DFF, D])
    norm_final = din("norm_final", [1, D])

    y_p = dout("y_p", [TPROMPT, D])
    y_s = dout("y_s", [NSEQ * TSAMP, D])
    o_pool_p = dout("o_pool_p", [NL, 15, D])
    o_conv_p = dout("o_conv_p", [NL, 3, D])
    o_lru_p = dout("o_lru_p", [NL, 1, D])
    o_pool_s = dout("o_pool_s", [NL, NSEQ * 15, D])
    o_conv_s = dout("o_conv_s", [NL, NSEQ * 3, D])
    o_lru_s = dout("o_lru_s", [NL, NSEQ, D])
    o_v_s = dout("o_v_s", [NL, NSEQ * TSAMP, 512])

    stack = ExitStack()

    def sb(name, shape, dt=F32):
        return stack.enter_context(nc.sbuf_tensor(name, list(shape), dt))

    xT = sb("xT", [P, KC, TMAX])
    hT = sb("hT", [P, KC, TMAX], BF16)
    ring = sb("ring", [P, NSLOT, SLOT_EL], BF16)
    modT = sb("modT", [P, NL, 72, 17])
    vecT = sb("vecT", [P, V_BLK * 128])
    ident = sb("ident", [P, P])
    onesf = sb("onesf", [P, P])
    onesb = sb("onesb", [P, P], BF16)
    wmT = sb("wmT", [P, 16, P], BF16)
    BD = sb("BD", [P, 16, P], BF16)
    bsb = sb("bsb", [P, 2, 4, P])
    gnb = sb("gnb", [P, 512])
    scT = sb("scT", [P, KC, 17], BF16)
    poolst = sb("poolst", [P, NL, KC, 15])
    convst = sb("convst", [P, NL, KC, 3])
    lrust = sb("lrust", [P, NL, KC])
    lruc = sb("lruc", [P, NL, 4, KC])
    rcnt = sb("rcnt", [P, 16])
    stg = sb("stg", [P, D])
    merged = sb("merged", [P, KC, TMAX], BF16)
    ybr = sb("ybr", [P, KC, TMAX], BF16)
    SCRW = 10240
    scr = sb("scr", [P, SCRW])
    psum = stack.enter_context(nc.psum_tensor("psum", [P, 8, 512], F32))

    def scrv(off_words, shape, dt=F32):
        n = int(np.prod(shape))
        words = n if dt == F32 else (n + 1) // 2
        assert off_words + words <= SCRW, (off_words, shape)
        v = scr[:, off_words:off_words + words]
        if dt != F32:
            v = v.bitcast(dt)
        if len(shape) == 1:
            return v
        names = " ".join("d%d" % i for i in range(len(shape)))
        kw = {"d%d" % i: shape[i] for i in range(1, len(shape))}
        return v.rearrange("p (%s) -> p %s" % (names, names), **kw)

    psrot = {"mm": [0, 1, 2, 3], "aux": [4, 5], "tr": [6, 7]}
    psidx = {"mm": 0, "aux": 0, "tr": 0}

    def ps(pool="mm"):
        b = psrot[pool][psidx[pool] % len(psrot[pool])]
        psidx[pool] += 1
        return psum[:, b, :]

    class WV:
        def __init__(self, ap, slot, gen):
            self.ap, self.slot, self.gen = ap, slot, gen

        def __getitem__(self, idx):
            return WV(self.ap[idx], self.slot, self.gen)

    ring_gen = [0] * NSLOT

    def unwrap(x):
        if isinstance(x, WV):
            assert ring_gen[x.slot] == x.gen, "weight ring slot reused before its consumer was recorded"
            return x.ap
        return x

    def mm(out, lhsT, rhs, start, stop):
        lhsT = unwrap(lhsT)
        rhs = unwrap(rhs)
        S.add("pe", lambda e: e.matmul(out, lhsT=lhsT, rhs=rhs, start=start, stop=stop),
              reads=[lhsT, rhs], writes=[out])

    def tr(out, in_):
        k = in_.shape[0]
        idn = ident[0:k, 0:k]
        S.add("pe", lambda e: e.transpose(out, in_, idn), reads=[in_, idn], writes=[out])

    def act(out, in_, func, bias=None, scale=None, accum_out=None, extra_r=(), extra_w=()):
        kw = {}
        if bias is not None:
            kw["bias"] = bias
        if scale is not None:
            kw["scale"] = scale
        if accum_out is not None:
            kw["accum_out"] = accum_out
        rd = [in_] + [a for a in (bias, scale) if a is not None and not isinstance(a, (int, float))] + list(extra_r)
        wr = [out] + ([accum_out] if accum_out is not None else []) + list(extra_w)
        S.add("act", lambda e: e.activation(out=out, in_=in_, func=func, **kw), reads=rd, writes=wr)

    def tt(out, in0, in1, op, eng="dve"):
        S.add(eng, lambda e: e.tensor_tensor(out=out, in0=in0, in1=in1, op=op), reads=[in0, in1], writes=[out])

    def ts(out, in0, s1, s2, op0, op1=None, eng="dve"):
        rd = [in0] + [a for a in (s1, s2) if a is not None and not isinstance(a, (int, float))]
        if op1 is None:
            S.add(eng, lambda e: e.tensor_scalar(out=out, in0=in0, scalar1=s1, scalar2=None, op0=op0),
                  reads=rd, writes=[out])
        else:
            S.add(eng, lambda e: e.tensor_scalar(out=out, in0=in0, scalar1=s1, scalar2=s2, op0=op0, op1=op1),
                  reads=rd, writes=[out])

    def stt(out, in0, scalar, in1, op0, op1, eng="dve"):
        rd = [in0, in1] + ([scalar] if not isinstance(scalar, (int, float)) else [])
        S.add(eng, lambda e: e.scalar_tensor_tensor(out=out, in0=in0, scalar=scalar, in1=in1, op0=op0, op1=op1),
              reads=rd, writes=[out])

    def cp(out, in_, eng="dve"):
        S.add(eng, lambda e: e.tensor_copy(out=out, in_=in_), reads=[in_], writes=[out])

    def memset(ap, val, eng="pool", wkeys=None):
        S.add(eng, lambda e: e.memset(ap, val), writes=[ap] if wkeys is None else wkeys)

    def recip(out, in_):
        S.add("dve", lambda e: e.reciprocal(out=out, in_=in_), reads=[in_], writes=[out])

    def scan(out, d0, d1, init):
        rd = [d0, d1] + ([init] if not isinstance(init, (int, float)) else [])
        S.add("dve", lambda e: e.tensor_tensor_scan(out=out, data0=d0, data1=d1, initial=init,
                                                    op0=ALU.mult, op1=ALU.add), reads=rd, writes=[out])

    def dma(out, in_, q="sp", reads=None, writes=None, slow=False):
        if slow:
            fn = lambda e: e.dma_start(out=out, in_=in_, allow_slow_non_contiguous=True)
        else:
            fn = lambda e: e.dma_start(out=out, in_=in_)
        S.add(q, fn, reads=[in_] if reads is None else reads, writes=[out] if writes is None else writes, dma=True)

    ringi = [0]
    ada_hook = [None]

    def wload(parts):
        s = ringi[0] % NSLOT
        ringi[0] += 1
        ring_gen[s] += 1
        views = []
        for (off, shape, src) in parts:
            n = int(np.prod(shape))
            assert off + n <= SLOT_EL
            v = ring[:, s, off:off + n]
            if len(shape) == 2:
                v = v.rearrange("p (a b) -> p a b", a=shape[0], b=shape[1])
            dma(v, src, q="pool")
            views.append(WV(v, s, ring_gen[s]))
        if ada_hook[0] is not None:
            ada_hook[0]()
        return views

    def wcols(W, c0, n):
        return W.rearrange("(k p) n -> p k n", p=P)[:, :, c0:c0 + n]

    def vcol(l, name, k):
        r = l * VPL + VR[name] + k
        return vecT[:, r:r + 1]

    def vcols(l, name, k0, n):
        r = l * VPL + VR[name] + k0
        return vecT[:, r:r + n]

    memset(onesf[:], 1.0)
    S.add("pool", lambda e: e.affine_select(ident[:], onesf[:], pattern=[[-1, P]], compare_op=ALU.is_equal,
                                            fill=0.0, base=0, channel_multiplier=1),
          reads=[onesf[:]], writes=[ident[:]])
    memset(onesb[:], 1.0 / D)
    memset(poolst[:], 0.0)
    memset(convst[:], 0.0)
    memset(lrust[:], 0.0)
    for t in range(16):
        memset(rcnt[:, t:t + 1], 1.0 / (t + 1))
    vstage = scrv(0, [V_BLK, P])
    vkeys = []
    memset(vstage, 0.0)
    vms = S.ops["pool"][-1]

    def vload(row0, src2d):
        n = src2d.shape[0]
        r = row0
        done = 0
        while done < n:
            blk, p0 = divmod(r, 128)
            m = min(n - done, 128 - p0)
            key = ("vst", len(vkeys))
            vkeys.append(key)
            dma(vstage[p0:p0 + m, blk, :], src2d[done:done + m, :], writes=[key])
            o = S.ops["sp"][-1]
            o.deps = list(o.deps) + [vms]
            vms.signal = True
            done += m
            r += m

    for l in range(NL):
        b0 = l * VPL
        vload(b0 + VR["b_ada"], b_ada[l].rearrange("(r c) -> r c", c=P))
        vload(b0 + VR["n1"], norm_ffn1[l].rearrange("(r c) -> r c", c=P))
        vload(b0 + VR["nm"], norm_mix[l].rearrange("(r c) -> r c", c=P))
        vload(b0 + VR["n2"], norm_ffn2[l].rearrange("(r c) -> r c", c=P))
        vload(b0 + VR["pscale"], pool_scale[l].rearrange("(r c) -> r c", c=P))
        vload(b0 + VR["convw"], conv_w[l].rearrange("k (r c) -> (k r) c", c=P))
        vload(b0 + VR["convb"], conv_b[l].rearrange("(r c) -> r c", c=P))
        vload(b0 + VR["br"], lru_br[l].rearrange("(r c) -> r c", c=P))
        vload(b0 + VR["bi"], lru_bi[l].rearrange("(r c) -> r c", c=P))
        vload(b0 + VR["lam"], lru_lambda[l].rearrange("(r c) -> r c", c=P))
    vload(V_NF, norm_final[0].rearrange("(r c) -> r c", c=P))
    for blk in range(V_BLK):
        pt = ps("tr")
        k_ = vstage[:, blk, :].shape[0]
        S.add("pe", (lambda pt=pt, blk=blk: lambda e: e.transpose(pt[:, 0:P], vstage[:, blk, :], ident[:]))(),
              reads=[vstage[:, blk, :], ident[:]] + vkeys, writes=[pt[:, 0:P]])
        act(vecT[:, blk * P:(blk + 1) * P], pt[:, 0:P], AF.Copy)
    for l in range(NL):
        lamv = vcols(l, "lam", 0, KC)
        e1 = scrv(4096, [KC])
        act(e1, lamv, AF.Exp, scale=-1.0)
        act(e1, e1, AF.Ln, bias=1.0)
        ts(lruc[:, l, 0, :], e1, -4.0, None, ALU.mult)
        ts(lruc[:, l, 1, :], e1, -8.0, None, ALU.mult)
        ts(lruc[:, l, 2, :], vcols(l, "br", 0, KC), 0.5, None, ALU.mult)
        ts(lruc[:, l, 3, :], vcols(l, "bi", 0, KC), 0.5, None, ALU.mult)
    c17s = scrv(0, [D])
    dma(c17s[0:17, :], c17_d, writes=[c17s[0:17, :]] + vkeys)
    ptc = ps("tr")
    for k in range(KC):
        tr(ptc[:, k * 17:(k + 1) * 17], c17s[0:17, k * P:(k + 1) * P])
    act(scT[:], ptc[:, 0:KC * 17].rearrange("p (k c) -> p k c", c=17), AF.Silu)
    wsst = scrv(0, [16, P])
    dma(wsst, gmlp_ws.rearrange("l g t s -> t (l g) s"))
    for lg in range(16):
        S.add("pool", (lambda lg: lambda e: e.affine_select(wsst[:, lg, :], wsst[:, lg, :], pattern=[[-1, P]],
                                                             compare_op=ALU.is_ge, fill=0.0, base=0,
                                                             channel_multiplier=1))(lg),
              reads=[wsst[:, lg, :]], writes=[wsst[:, lg, :]])
        pt = ps("tr")
        tr(pt[:, 0:P], wsst[:, lg, :])
        act(wmT[:, lg, :], pt[:, 0:P], AF.Copy)

    def ada_slot(l, s):
        (wv,) = wload([(0, [KC, 512], wcols(w_ada[l], s * 512, 512))])
        pa = ps("tr")
        for m4 in range(4):
            for k in range(KC):
                mm(pa[:, m4 * 17:(m4 + 1) * 17], wv[:, k, m4 * P:(m4 + 1) * P], scT[:, k, :], k == 0, k == KC - 1)
        for m4 in range(4):
            m = s * 4 + m4
            act(modT[:, l, m, :], pa[:, m4 * 17:(m4 + 1) * 17], AF.Identity, bias=vcol(l, "b_ada", m), scale=1.0)

    def ada_post(l, whs):
        for (wh, nm) in ((1, "n1"), (4, "nm"), (7, "n2")):
            if wh not in whs:
                continue
            sl = modT[:, l, wh * 8:(wh + 1) * 8, :]
            nb = vcols(l, nm, 0, KC).unsqueeze(2).to_broadcast([P, KC, 17])
            stt(sl, sl, 1.0, nb, ALU.add, ALU.mult)
        for wh in (2, 5, 8):
            if wh not in whs:
                continue
            sl = modT[:, l, wh * 8:(wh + 1) * 8, :]
            ts(sl, sl, 0.5, None, ALU.mult)

    ada_q = []
    ada_up = [(lambda s_=s_: ada_slot(0, s_)) for s_ in range(6)] + [lambda: ada_post(0, (1, 2))]
    for s_ in range(6, 10):
        ada_q.append((0, 0, (lambda s_=s_: ada_slot(0, s_))))
    ada_q.append((0, 0, (lambda: ada_post(0, (4,)))))
    for s_ in range(10, 18):
        ada_q.append((0, 1, (lambda s_=s_: ada_slot(0, s_))))
    ada_q.append((0, 1, (lambda: ada_post(0, (5, 7, 8)))))
    for l_ in range(1, NL):
        for s_ in range(18):
            ada_q.append((l_, 0, (lambda l_=l_, s_=s_: ada_slot(l_, s_))))
        ada_q.append((l_, 0, (lambda l_=l_: ada_post(l_, (1, 2, 4, 5, 7, 8)))))
    ada_ctl = {"on": False, "lmax": 0, "tick": 0, "busy": False}

    def ada_tick():
        if not ada_ctl["on"] or ada_ctl["busy"] or not ada_q:
            return
        l_, st, fn = ada_q[0]
        if l_ > ada_ctl["lmax"]:
            return
        ada_ctl["tick"] += 1
        if l_ > 0 and ada_ctl["tick"] % 2:
            return
        ada_q.pop(0)
        ada_ctl["busy"] = True
        fn()
        ada_ctl["busy"] = False

    def ada_drain(l, st):
        while ada_q and (ada_q[0][0], ada_q[0][1]) <= (l, st):
            fn = ada_q.pop(0)[2]
            ada_ctl["busy"] = True
            fn()
            ada_ctl["busy"] = False

    ada_hook[0] = ada_tick

    def modp(l, wh, k):
        return modT[:, l, wh * 8 + k, 0:1]

    def mods(l, wh, k0=0, nk=KC):
        return modT[:, l, wh * 8 + k0:wh * 8 + k0 + nk, 1:17].unsqueeze(3).to_broadcast([P, nk, NSEQ, TSAMP])

    bdkeys = [("BD", lg, j) for lg in range(16) for j in range(16)]
    memset(BD[:], 0.0, wkeys=bdkeys)

    def build_bd():
        for lg in range(16):
            for j in range(16):
                dma(BD[8 * j:8 * j + 8, lg, 8 * j:8 * j + 8], wmT[0:8, lg, 0:8],
                    reads=[wmT[:, lg, :]], writes=[("BD", lg, j)])

    def load_inputs(grp):
        xst = scrv(0, [4, D])
        for t in grp:
            if t.kind == "p":
                dma(xst, xp_d[t.pos0:t.pos0 + 512, :].rearrange("(s p) d -> p s d", p=P))
                nsub = 4
            else:
                dma(xst[:, 0, :], xs_d)
                nsub = 1
            for k in range(KC):
                pt = ps("tr")
                for s in range(nsub):
                    tr(pt[:, s * P:(s + 1) * P], xst[:, s, k * P:(k + 1) * P])
                act(xT[:, k, t.col:t.col + t.n], pt[:, 0:t.n], AF.Copy)

    nrm_i = [0]

    def rstd_stage1(t):
        n = t.n
        par = nrm_i[0] % 2
        nrm_i[0] += 1
        sq = scrv(par * 2048, [KC, 512], BF16)
        rstd = scrv(4096 + par * 512, [512])
        act(sq[:, :, 0:n], xT[:, :, t.col:t.col + n], AF.Square)

        def stage2():
            pb = ps("aux")
            for k in range(KC):
                mm(pb[:, 0:n], onesb[:], sq[:, k, 0:n], k == 0, k == KC - 1)
            act(rstd[:, 0:n], pb[:, 0:n], AF.Ln, bias=EPS, scale=1.0)
            act(rstd[:, 0:n], rstd[:, 0:n], AF.Exp, scale=-0.5)
            return rstd[:, 0:n]
        return stage2

    def rstd_tile(t):
        return rstd_stage1(t)()

    def norm_apply(l, which, t, rstd):
        wsh, wA = which * 3, which * 3 + 1
        n = t.n
        if t.kind == "p":
            for k in range(KC):
                tmp = scrv(5120 + (k % 4) * 512, [512])
                tt(tmp[:, 0:n], xT[:, k, t.col:t.col + n], rstd, ALU.mult)
                act(hT[:, k, t.col:t.col + n], tmp[:, 0:n], AF.Identity, bias=modp(l, wsh, k), scale=modp(l, wA, k))
        else:
            tmp = scrv(5120, [KC, P])
            tt(tmp, xT[:, :, t.col:t.col + n], rstd.unsqueeze(1).to_broadcast([P, KC, n]), ALU.mult)
            t4 = tmp.rearrange("p k (s t) -> p k s t", t=TSAMP)
            tt(t4, t4, mods(l, wA), ALU.mult)
            tt(hT[:, :, t.col:t.col + n].rearrange("p k (s t) -> p k s t", t=TSAMP), t4, mods(l, wsh), ALU.add)

    def norm_stage1(l, which, t):
        s2 = rstd_stage1(t)
        return lambda: norm_apply(l, which, t, s2())

    def norm_mod(l, which, grp):
        for t in grp:
            norm_stage1(l, which, t)()

    def tile_outer(grp, body, after1):
        pend = None
        for t in grp:
            for m in range(KC):
                body(t, m)
                if m == 1 and pend is not None:
                    pend()
                    pend = None
            pend = after1(t) if after1 is not None else None
        if pend is not None:
            pend()

    def resid_update(l, wh, m, t, pb):
        n = t.n
        xs_ = xT[:, m, t.col:t.col + n]
        if t.kind == "p":
            stt(xs_, pb[:, 0:n], modp(l, wh, m), xs_, ALU.mult, ALU.add)
        else:
            tmp = scrv(9984, [P])
            g3 = modT[:, l, wh * 8 + m, 1:17].unsqueeze(2).to_broadcast([P, NSEQ, TSAMP])
            tt(tmp.rearrange("p (s t) -> p s t", t=TSAMP), pb[:, 0:n].rearrange("p (s t) -> p s t", t=TSAMP), g3, ALU.mult)
            tt(xs_, xs_, tmp, ALU.add)

    def ffn(l, which, grp, ada_l=None, after1=None):
        Wg, Wu, Wd = ((f1g, f1u, f1d), (f2g, f2u, f2d))[which]
        wh_gt = 2 if which == 0 else 8
        actb = merged
        for half in range(2):
            j0 = half * HALF_FF
            for jj in range(0, HALF_FF, 2):
                nj = min(2, HALF_FF - jj)
                c0 = (j0 + jj) * P
                wg, wu = wload([(0, [KC, nj * P], wcols(Wg[l], c0, nj * P)),
                                (2048, [KC, nj * P], wcols(Wu[l], c0, nj * P))])
                order = [(ji, t) for ji in range(nj) for t in grp]
                if half == 0 and jj == 0:
                    order = [(ji, t) for t in grp for ji in range(nj)]
                for (ji, t) in order:
                    jl = jj + ji
                    if True:
                        n = t.n
                        rhs_cols = slice(t.col, t.col + n)
                        pg = ps("mm")
                        for k in range(KC):
                            mm(pg[:, 0:n], wg[:, k, ji * P:(ji + 1) * P], hT[:, k, rhs_cols], k == 0, k == KC - 1)
                        pu = ps("mm")
                        for k in range(KC):
                            mm(pu[:, 0:n], wu[:, k, ji * P:(ji + 1) * P], hT[:, k, rhs_cols], k == 0, k == KC - 1)
                        sg = scrv((ffsg[0] % 4) * 512, [512])
                        ffsg[0] += 1
                        act(sg[:, 0:n], pg[:, 0:n], AF.Silu)
                        tt(ffact(jl)[:, rhs_cols], sg[:, 0:n], pu[:, 0:n], ALU.mult)
            if half == 0:
                for m2 in range(0, KC, 2):
                    (wd,) = wload([(0, [HALF_FF, 2 * P],
                                    Wd[l].rearrange("(j p) n -> p j n", p=P)[:, j0:j0 + HALF_FF, m2 * P:(m2 + 2) * P])])
                    for mi in range(2):
                        m = m2 + mi
                        for t in grp:
                            n = t.n
                            py = ps("mm")
                            for jl in range(HALF_FF):
                                mm(py[:, 0:n], wd[:, jl, mi * P:(mi + 1) * P], ffact(jl)[:, t.col:t.col + n],
                                   jl == 0, jl == HALF_FF - 1)
                            resid_update(l, wh_gt, m, t, py)
            else:
                wds = []
                hold = ada_ctl["busy"]
                ada_ctl["busy"] = True
                for m2 in range(0, KC, 2):
                    (wd,) = wload([(0, [HALF_FF, 2 * P],
                                    Wd[l].rearrange("(j p) n -> p j n", p=P)[:, j0:j0 + HALF_FF, m2 * P:(m2 + 2) * P])])
                    wds.append(wd)
                ada_ctl["busy"] = hold

                def body(t, m):
                    n = t.n
                    wd = wds[m // 2]
                    mi = m % 2
                    py = ps("mm")
                    for jl in range(HALF_FF):
                        mm(py[:, 0:n], wd[:, jl, mi * P:(mi + 1) * P], ffact(jl)[:, t.col:t.col + n],
                           jl == 0, jl == HALF_FF - 1)
                    resid_update(l, wh_gt, m, t, py)
                if which == 0:
                    ada_drain(l, 0)
                else:
                    ada_drain(l + 1, 1)
                tile_outer(grp, body, after1)

    ffsg = [0]

    def ffact(jl):
        if jl < 8:
            return merged[:, jl, :]
        return ybr[:, jl - 8, :]

    gmi = [0]

    def gate_merge(l, grp, br, Wbo, nk):
        for s4 in range(2):
            (wb,) = wload([(0, [nk, 512], wcols(Wbo[l], s4 * 512, 512))])
            (wgt,) = wload([(0, [KC, 512], wcols(w_in[l], O_G0 + br * D + s4 * 512, 512))])
            for m4 in range(4):
                m = s4 * 4 + m4
                for t in grp:
                    n = t.n
                    cols = slice(t.col, t.col + n)
                    pp = ps("mm")
                    for k in range(nk):
                        mm(pp[:, 0:n], wb[:, k, m4 * P:(m4 + 1) * P], ybr[:, k, cols], k == 0, k == nk - 1)
                    pgt = ps("aux")
                    for k in range(KC):
                        mm(pgt[:, 0:n], wgt[:, k, m4 * P:(m4 + 1) * P], hT[:, k, cols], k == 0, k == KC - 1)
                    i = gmi[0]
                    gmi[0] += 1
                    sg = scrv(8192 + (i % 2) * 512, [512])
                    act(sg[:, 0:n], pgt[:, 0:n], AF.Tanh, scale=0.5)
                    if br == 0:
                        stt(merged[:, m, cols], sg[:, 0:n], 1.0, pp[:, 0:n], ALU.add, ALU.mult)
                    else:
                        tmp = scrv(9216 + (i % 2) * 512, [512])
                        stt(tmp[:, 0:n], sg[:, 0:n], 1.0, pp[:, 0:n], ALU.add, ALU.mult)
                        tt(merged[:, m, cols], merged[:, m, cols], tmp[:, 0:n], ALU.add)

    def store_rows(src_fn, ncols, dram_rows, ):
        for kb in range(2):
            pt = ps("tr")
            for k4 in range(4):
                tr(pt[0:ncols, k4 * P:(k4 + 1) * P], src_fn(kb * 4 + k4))
            act(stg[0:ncols, kb * 512:(kb + 1) * 512], pt[0:ncols, :], AF.Copy)
        dma(dram_rows, stg[0:ncols, :])

    def load_rows_T(dram_rows, nrows, dst_fn):
        dma(stg[0:nrows, :], dram_rows)
        for k in range(KC):
            pt = ps("tr")
            tr(pt[:, 0:nrows], stg[0:nrows, k * P:(k + 1) * P])
            act(dst_fn(k), pt[:, 0:nrows], AF.Copy)

    def mixer(l, grp, gi, after1=None):
        has_s = any(t.kind == "s" for t in grp)
        last_prompt = (gi == len(GROUPS) - 1)
        dma(bsb[:, 0], gmlp_bs[l].partition_broadcast(P))
        for g_ in range(4):
            dma(bsb[:, 1, g_, :].rearrange("p (j t) -> p j t", t=TSAMP),
                gmlp_bs[l, g_, 0:TSAMP].partition_broadcast(P).unsqueeze(1).to_broadcast([P, NSEQ, TSAMP]), slow=True)
        dma(gnb[:], gmlp_norm[l].partition_broadcast(P))
        spT = scrv(0, [KC, NSEQ * 15])
        scvT = scrv(1920, [KC, NSEQ * 3])
        h0T = scrv(2304, [KC, NSEQ])
        hTs = scrv(2432, [KC, NSEQ])
        BASE = 2560
        def load_pool_state():
            for hh in range(2):
                load_rows_T(spool_d[l, hh * 120:(hh + 1) * 120, :], 120,
                            lambda k, hh=hh: spT[:, k, hh * 120:(hh + 1) * 120])

        def load_lru_state():
            load_rows_T(sconv_d[l], 48, lambda k: scvT[:, k, :])
            load_rows_T(slru_d[l], 16, lambda k: h0T[:, k, :])

        def pool_A(g, t, par, wx, wp):
            w = 2 << g
            n, nseq, tl = t.n, t.nseq, t.tl
            L = 15 + tl
            U = BASE + par * 3712
            xcat = scrv(U, [2, nseq, L])
            wbufs = [scrv(U + 1056, [2, nseq, L]), scrv(U + 2112, [2, nseq, L])]
            dbuf = scrv(U + 3168, [2, nseq, tl], BF16)
            for c in range(2):
                ch = g * 2 + c
                if t.kind == "p":
                    cp(xcat[:, c, 0, 0:15], poolst[:, l, ch, :])
                else:
                    cp(xcat[:, c, :, 0:15], spT[:, ch, :].rearrange("p (s r) -> p s r", r=15))
                pb = ps("mm")
                for k in range(KC):
                    mm(pb[:, 0:n], wx[:, k, c * P:(c + 1) * P], hT[:, k, t.col:t.col + n], k == 0, k == KC - 1)
                act(xcat[:, c, :, 15:L], pb[:, 0:n].rearrange("p (s t) -> p s t", t=tl), AF.Copy)
                if t.kind == "p":
                    cp(poolst[:, l, ch, :], xcat[:, c, 0, L - 15:L])
                else:
                    cp(spT[:, ch, :].rearrange("p (s r) -> p s r", r=15), xcat[:, c, :, L - 15:L])
            cur, curlen = xcat, L
            for i in range(g + 1):
                sh = 1 << i
                o = wbufs[i % 2]
                tt(o[:, :, :, 0:curlen - sh], cur[:, :, :, sh:curlen], cur[:, :, :, 0:curlen - sh], ALU.add)
                cur, curlen = o, curlen - sh
            off = 16 - w
            stt(dbuf[:], cur[:, :, :, off:off + tl], 1.0 / w, xcat[:, :, :, 15:L], ALU.mult, ALU.subtract)
            if t.kind == "p" and t.pos0 == 0:
                nf = w - 1
                fx = scrv(U + 3680, [2, 1, 16])
                tt(fx[:, :, :, 0:nf], cur[:, :, :, off:off + nf],
                   rcnt[:, 0:nf].unsqueeze(1).unsqueeze(1).to_broadcast([P, 2, 1, nf]), ALU.mult)
                tt(dbuf[:, :, :, 0:nf], fx[:, :, :, 0:nf], xcat[:, :, :, 15:15 + nf], ALU.subtract)

            def stage_b():
                for m in range(2):
                    pb2 = ps("aux")
                    for k in range(2):
                        mm(pb2[:, 0:n], wp[:, k, m * P:(m + 1) * P],
                           dbuf[:, k].rearrange("p s t -> p (s t)"), k == 0, k == 1)
                    act(ybr[:, g * 2 + m, t.col:t.col + n], pb2[:, 0:n], AF.Identity,
                        scale=vcol(l, "pscale", g * 2 + m), bias=0.0)
            return stage_b

        pend = None
        ui = 0
        for gp in range(2):
            wts = {}
            hold = ada_ctl["busy"]
            ada_ctl["busy"] = True
            for g in (2 * gp, 2 * gp + 1):
                wts[g] = wload([(0, [KC, 256], wcols(w_in[l], O_XP + g * 256, 256)),
                                (2048, [2, 256], pool_w[l, g].rearrange("(k p) n -> p k n", p=P))])
            ada_ctl["busy"] = hold
            for t in grp:
                if t.kind == "s" and gp == 0:
                    load_pool_state()
                for g in (2 * gp, 2 * gp + 1):
                    wx, wp = wts[g]
                    nb = pool_A(g, t, ui % 2, wx, wp)
                    ui += 1
                    if pend is not None:
                        pend()
                    pend = nb
        pend()
        if has_s:
            for hh in range(2):
                store_rows(lambda k, hh=hh: spT[:, k, hh * 120:(hh + 1) * 120], 120,
                           o_pool_s[l, hh * 120:(hh + 1) * 120, :])
        if last_prompt:
            store_rows(lambda k: poolst[:, l, k, :], 15, o_pool_p[l])
        gate_merge(l, grp, 0, wbo_pool, KC)

        def lru_A(ch, t, par, wxl, wgl, wr_, wi_):
            n, nseq, tl = t.n, t.nseq, t.tl
            L = 3 + tl
            U = BASE + par * 3344
            xlc = scrv(U, [nseq, L])
            xc = scrv(U + 528, [nseq, tl])
            xcb = scrv(U + 1040, [512], BF16)
            trb = scrv(U + 1296, [512])
            tib = scrv(U + 1808, [512])
            aa = scrv(U + 2320, [512])
            sq = scrv(U + 2832, [512])
            gel = scrv(par * 512, [512])
            t16 = scrv(1024 + par * 16, [NSEQ])
            cols = slice(t.col, t.col + n)
            if t.kind == "p":
                cp(xlc[:, 0, 0:3], convst[:, l, ch, :])
            else:
                cp(xlc[:, :, 0:3], scvT[:, ch, :].rearrange("p (s r) -> p s r", r=3))
            pb = ps("mm")
            for k in range(KC):
                mm(pb[:, 0:n], wxl[:, k, :], hT[:, k, cols], k == 0, k == KC - 1)
            cp(xlc[:, :, 3:L], pb[:, 0:n].rearrange("p (s t) -> p s t", t=tl))
            if t.kind == "p":
                cp(convst[:, l, ch, :], xlc[:, 0, L - 3:L])
            else:
                cp(scvT[:, ch, :].rearrange("p (s r) -> p s r", r=3), xlc[:, :, L - 3:L])
            cw = lambda kk: vcol(l, "convw", kk * 8 + ch)
            ts(xc[:], xlc[:, :, 0:tl], cw(0), vcol(l, "convb", ch), ALU.mult, ALU.add)
            for kk in range(1, 4):
                stt(xc[:], xlc[:, :, kk:kk + tl], cw(kk), xc[:], ALU.mult, ALU.add)
            xcf = xc.rearrange("p s t -> p (s t)")
            cp(xcb[:, 0:n], xcf)
            pg = ps("mm")
            for k in range(KC):
                mm(pg[:, 0:n], wgl[:, k, :], hT[:, k, cols], k == 0, k == KC - 1)

            def stage_b():
                pr = ps("aux")
                mm(pr[:, 0:n], wr_, xcb[:, 0:n], True, True)
                pi = ps("aux")
                mm(pi[:, 0:n], wi_, xcb[:, 0:n], True, True)
                act(gel[:, 0:n], pg[:, 0:n], AF.Gelu_apprx_tanh)
                act(trb[:, 0:n], pr[:, 0:n], AF.Tanh, bias=lruc[:, l, 2, ch:ch + 1], scale=0.5)
                act(tib[:, 0:n], pi[:, 0:n], AF.Tanh, bias=lruc[:, l, 3, ch:ch + 1], scale=0.5)
                act(aa[:, 0:n], trb[:, 0:n], AF.Exp, scale=lruc[:, l, 0, ch:ch + 1], bias=lruc[:, l, 0, ch:ch + 1])
                act(sq[:, 0:n], trb[:, 0:n], AF.Exp, scale=lruc[:, l, 1, ch:ch + 1], bias=lruc[:, l, 1, ch:ch + 1])
                act(sq[:, 0:n], sq[:, 0:n], AF.Ln, scale=-1.0, bias=1.0)
                act(sq[:, 0:n], sq[:, 0:n], AF.Exp, scale=0.5)
                stt(tib[:, 0:n], tib[:, 0:n], 1.0, xcf, ALU.add, ALU.mult)
                stt(tib[:, 0:n], tib[:, 0:n], 0.5, sq[:, 0:n], ALU.mult, ALU.mult)
                if t.kind == "p":
                    scan(xcf, aa[:, 0:n], tib[:, 0:n], lrust[:, l, ch:ch + 1])
                    cp(lrust[:, l, ch:ch + 1], xcf[:, n - 1:n])
                else:
                    tt(t16[:], aa[:, 0:n:TSAMP], h0T[:, ch, :], ALU.mult)
                    tt(tib[:, 0:n:TSAMP], tib[:, 0:n:TSAMP], t16[:], ALU.add)
                    memset(aa[:, 0:n:TSAMP], 0.0, eng="dve")
                    scan(xcf, aa[:, 0:n], tib[:, 0:n], 0.0)
                    cp(hTs[:, ch, :], xcf[:, TSAMP - 1:n:TSAMP])
                tt(ybr[:, ch, cols], xcf, gel[:, 0:n], ALU.mult)
            return stage_b

        pend = None
        ui = 0
        for ch in range(KC):
            wxl, wgl, wr_, wi_ = wload([(0, [KC, P], wcols(w_in[l], O_XL + ch * P, P)),
                                        (1024, [KC, P], wcols(w_in[l], O_GL + ch * P, P)),
                                        (2048, [P], lru_wr[l, ch]),
                                        (2176, [P], lru_wi[l, ch])])
            for t in grp:
                if t.kind == "s" and ch == 0:
                    load_lru_state()
                nb = lru_A(ch, t, ui % 2, wxl, wgl, wr_, wi_)
                ui += 1
                if pend is not None:
                    pend()
                pend = nb
        pend()
        if has_s:
            store_rows(lambda k: scvT[:, k, :], 48, o_conv_s[l])
            store_rows(lambda k: hTs[:, k, :], 16, o_lru_s[l])
        if last_prompt:
            store_rows(lambda k: convst[:, l, k, :], 3, o_conv_p[l])
            store_rows(lambda k: lrust[:, l, k:k + 1], 1, o_lru_p[l])
        gate_merge(l, grp, 1, wbo_lru, KC)

        (wv_,) = wload([(0, [KC, 512], wcols(w_in[l], O_V, 512))])
        sall = scrv(BASE, [4, TMAX])
        vjunk = scrv(BASE + 4608, [512], BF16)

        def gm_A(t, sub, par):
            c0 = t.col + sub * P
            vnb = scrv(BASE + 4864 + par * 256, [512], BF16)
            vnf = scrv(1024, [512])
            ssv = scrv(1536 + par * 4, [4])
            pv = ps("mm")
            for k in range(KC):
                mm(pv[:, :], hT[:, k, c0:c0 + P], wv_[:, k, :], k == 0, k == KC - 1)
            act(vjunk[:], pv[:, :], AF.Square, accum_out=ssv[:, 0:1])
            act(ssv[:, 1:2], ssv[:, 0:1], AF.Ln, scale=1.0 / 512, bias=EPS)
            act(ssv[:, 1:2], ssv[:, 1:2], AF.Exp, scale=-0.5)
            if t.kind == "p":
                stt(vnb[:], pv[:, :], ssv[:, 1:2], gnb[:], ALU.mult, ALU.mult)
            else:
                stt(vnf[:], pv[:, :], ssv[:, 1:2], gnb[:], ALU.mult, ALU.mult)
                act(vnb[:], vnf[:], AF.Copy)
                dma(o_v_s[l], vnf[:])

            def stage_b():
                psx = ps("aux")
                for g in range(4):
                    lg = l * 4 + g
                    o_ = psx[:, g * P:(g + 1) * P]
                    l_ = vnb[:, g * P:(g + 1) * P]
                    if t.kind == "p":
                        mm(o_, l_, wmT[:, lg, :], True, True)
                    else:
                        b_ = BD[:, lg, :]
                        S.add("pe", (lambda o_=o_, l_=l_, b_=b_: lambda e: e.matmul(o_, lhsT=l_, rhs=b_, start=True, stop=True))(),
                              reads=[l_] + [("BD", lg, j) for j in range(16)], writes=[o_])
                tt(sall[:, :, c0:c0 + P], psx[:, :].rearrange("p (g t) -> p g t", t=P),
                   bsb[:, 0 if t.kind == "p" else 1], ALU.add)
            return stage_b

        pend = None
        si = 0
        for t in grp:
            for sub in range(t.n // P):
                nb = gm_A(t, sub, si % 2)
                si += 1
                if pend is not None:
                    pend()
                pend = nb
        pend()
        (wu_,) = wload([(0, [KC, 512], wcols(w_in[l], O_U, 512))])
        for g in range(4):
            for t in grp:
                n = t.n
                cols = slice(t.col, t.col + n)
                pu = ps("mm")
                for k in range(KC):
                    mm(pu[:, 0:n], wu_[:, k, g * P:(g + 1) * P], hT[:, k, cols], k == 0, k == KC - 1)
                tt(ybr[:, g, cols], pu[:, 0:n], sall[:, g, cols], ALU.mult)
        gate_merge(l, grp, 2, wbo_gmlp, 4)

        ada_drain(l, 1)
        wos = []
        for s4 in range(2):
            (wo,) = wload([(0, [KC, 512], wcols(w_out[l], s4 * 512, 512))])
            wos.append(wo)

        def body(t, m):
            n = t.n
            wo = wos[m // 4]
            m4 = m % 4
            py = ps("mm")
            for k in range(KC):
                mm(py[:, 0:n], wo[:, k, m4 * P:(m4 + 1) * P], merged[:, k, t.col:t.col + n], k == 0, k == KC - 1)
            resid_update(l, 5, m, t, py)
        tile_outer(grp, body, after1)

    def final_stage1(t):
        s2 = rstd_stage1(t)

        def stage2():
            yT = scrv(5120, [KC, 512])
            n = t.n
            rstd = s2()
            for k in range(KC):
                stt(yT[:, k, 0:n], xT[:, k, t.col:t.col + n], vecT[:, V_NF + k:V_NF + k + 1], rstd, ALU.mult, ALU.mult)
            for sub in range(n // P):
                for kb in range(2):
                    pt = ps("tr")
                    for k4 in range(4):
                        tr(pt[:, k4 * P:(k4 + 1) * P], yT[:, kb * 4 + k4, sub * P:(sub + 1) * P])
                    act(stg[:, kb * 512:(kb + 1) * 512], pt[:, :], AF.Copy)
                if t.kind == "p":
                    r0 = t.pos0 + sub * P
                    dma(y_p[r0:r0 + P, :], stg[:, :])
                else:
                    dma(y_s, stg[:, :])
        return stage2

    for gi, grp in enumerate(GROUPS):
        load_inputs(grp)
        if gi == 0:
            for fn in ada_up:
                fn()
            ada_ctl["on"] = True
        if gi == 1:
            build_bd()
        norm_mod(0, 0, grp)
        for l in range(NL):
            ada_ctl["lmax"] = l + 1
            ffn(l, 0, grp, None, after1=lambda t, l=l: norm_stage1(l, 1, t))
            mixer(l, grp, gi, after1=lambda t, l=l: norm_stage1(l, 2, t))
            if l + 1 < NL:
                ffn(l, 1, grp, None, after1=lambda t, l=l: norm_stage1(l + 1, 0, t))
            else:
                ffn(l, 1, grp, None, after1=final_stage1)
        assert not ada_q

    S.finalize(nc, stack)
    with nc.Block() as block:
        @block.tensor
        def _(e):
            S.emit("pe", e)

        @block.scalar
        def _(e):
            S.emit("act", e)

        @block.vector
        def _(e):
            S.emit("dve", e)

        @block.gpsimd
        def _(e):
            S.emit("pool", e)

        @block.sync
        def _(e):
            S.emit("sp", e)
    stack.close()
    return nc


_WNAMES = ["w_ada", "b_ada", "norm_ffn1", "ffn1_w_gate", "ffn1_w_up", "ffn1_w_down", "norm_mix", "w_in",
           "pool_w", "pool_scale", "conv_w", "conv_b", "lru_wr", "lru_br", "lru_wi", "lru_bi", "lru_lambda",
           "gmlp_norm", "gmlp_ws", "gmlp_bs", "wbo_pool", "wbo_lru", "wbo_gmlp", "w_out", "norm_ffn2",
           "ffn2_w_gate", "ffn2_w_up", "ffn2_w_down"]


def kernel(**inp):
    f = lambda a: np.ascontiguousarray(np.asarray(a, dtype=np.float32))
    nc = build_nc()
    shared = {n: f(inp[n]) for n in _WNAMES}
    shared["norm_final"] = f(inp["norm_final"]).reshape(1, D)
    xp = f(inp["x_prompt"]); xs = f(inp["x_sample"])
    cp_ = f(inp["c_prompt"]); cs = f(inp["c_sample"])
    sp = f(inp["state_pool"]); sc = f(inp["state_conv"]); sl = f(inp["state_lru"])
    in_maps = []
    for i in range(NCORES):
        sq = slice(i * NSEQ, (i + 1) * NSEQ)
        m = dict(shared)
        m["xp"] = xp[i]
        m["xs"] = xs[sq].reshape(NSEQ * TSAMP, D)
        m["c17"] = np.concatenate([cp_[i:i + 1], cs[sq]], axis=0)
        m["spool"] = sp[:, sq].reshape(NL, NSEQ * 15, D)
        m["sconv"] = sc[:, sq].reshape(NL, NSEQ * 3, D)
        m["slru"] = sl[:, sq].reshape(NL, NSEQ, D)
        in_maps.append(m)
    res = run_bass_kernel_spmd(nc, in_maps, core_ids=list(range(NCORES)))
    R = res.results
    y_prompt = np.stack([R[i]["y_p"] for i in range(NCORES)], axis=0)
    y_sample = np.concatenate([R[i]["y_s"].reshape(NSEQ, TSAMP, D) for i in range(NCORES)], axis=0)
    npp = np.stack([R[i]["o_pool_p"] for i in range(NCORES)], axis=1)
    ncp = np.stack([R[i]["o_conv_p"] for i in range(NCORES)], axis=1)
    nlp = np.stack([R[i]["o_lru_p"].reshape(NL, D) for i in range(NCORES)], axis=1)
    nps = np.concatenate([R[i]["o_pool_s"].reshape(NL, NSEQ, 15, D) for i in range(NCORES)], axis=1)
    ncs = np.concatenate([R[i]["o_conv_s"].reshape(NL, NSEQ, 3, D) for i in range(NCORES)], axis=1)
    nls = np.concatenate([R[i]["o_lru_s"] for i in range(NCORES)], axis=1)
    nvs = np.concatenate([R[i]["o_v_s"].reshape(NL, NSEQ, TSAMP, 512) for i in range(NCORES)], axis=1)
    return (y_prompt, y_sample, npp, ncp, nlp, nps, ncs, nls, nvs)
```

```python
import numpy as np
from contextlib import ExitStack
import concourse.bass as bass
import concourse.mybir as mybir
from concourse.bass_utils import run_bass_kernel_spmd

F32 = mybir.dt.float32
BF16 = mybir.dt.bfloat16
AF = mybir.ActivationFunctionType
ALU = mybir.AluOpType

NCORES = 8
P = 128
D = 1024
KC = 8
DFF = 2816
NL = 4
TPROMPT = 2048
NSEQ = 16
TSAMP = 8
EPS = 1e-6
PAGE = 512
NSLOT = 4
SLOT_EL = 4096
TMAX = 1152
HALF_FF = 11

O_XP, O_XL, O_GL, O_U, O_V, O_G0 = 0, 1024, 2048, 3072, 3584, 4096

VR = {"b_ada": 0, "n1": 72, "nm": 80, "n2": 88, "pscale": 96, "convw": 104, "convb": 136,
      "br": 144, "bi": 152, "lam": 160}
VPL = 168
V_NF = NL * VPL
V_ROWS = V_NF + 8
V_BLK = (V_ROWS + 127) // 128


class TT_:
    def __init__(self, kind, col, n, pos0):
        self.kind, self.col, self.n, self.pos0 = kind, col, n, pos0
        self.nseq = 1 if kind == "p" else NSEQ
        self.tl = n if kind == "p" else TSAMP


GROUPS = [
    [TT_("p", 0, 512, 0), TT_("p", 512, 512, 512)],
    [TT_("p", 0, 512, 1024), TT_("p", 512, 512, 1536), TT_("s", 1024, 128, 0)],
]


def _esize(dt):
    return mybir.dt.size(dt)


class Op:
    __slots__ = ("eng", "fn", "dma", "deps", "signal", "count", "sem", "target", "prev")

    def __init__(self, eng, fn, dma):
        self.eng, self.fn, self.dma = eng, fn, dma
        self.deps = ()
        self.signal = False
        self.count = 0
        self.sem = None
        self.target = 0
        self.prev = 0


class Sched:
    ENG = ("pe", "act", "dve", "pool", "sp")

    def __init__(self):
        self.ops = {e: [] for e in self.ENG}
        self.lw = {}
        self.rd = {}
        self.outs = []

    @staticmethod
    def keys(ap):
        if isinstance(ap, tuple):
            return [ap]
        if str(ap.space) == "DRAM":
            return None
        dims = list(ap.ap)
        es = _esize(ap.dtype)
        pstride = dims[0][0]
        name = ap.tensor.name
        base = (ap.offset % pstride) * es if pstride > 0 else ap.offset * es
        free = dims[1:]
        if not free:
            return [(name, base // PAGE)]
        outer = free[:-1]
        ls, lc = free[-1]
        run = (lc - 1) * abs(ls) * es + es
        starts = [base]
        for (s, c) in outer:
            if s == 0 or c == 1:
                continue
            starts = [b + i * s * es for b in starts for i in range(c)]
        pages = set()
        for st in starts:
            for pg in range(st // PAGE, (st + run - 1) // PAGE + 1):
                pages.add(pg)
        return [(name, pg) for pg in pages]

    def add(self, eng, fn, reads=(), writes=(), dma=False):
        o = Op(eng, fn, dma)
        deps = set()
        rkeys, wkeys = [], []
        for r in reads:
            if r is None or isinstance(r, (int, float)):
                continue
            k = self.keys(r)
            if k:
                rkeys.extend(k)
        isout = False
        for w in writes:
            k = self.keys(w)
            if k is None:
                isout = True
            else:
                wkeys.extend(k)
        for k in rkeys:
            w = self.lw.get(k)
            if w is not None:
                deps.add(w)
        for k in wkeys:
            w = self.lw.get(k)
            if w is not None:
                deps.add(w)
            for r in self.rd.get(k, ()):
                deps.add(r)
        for k in rkeys:
            self.rd.setdefault(k, []).append(o)
        for k in wkeys:
            self.lw[k] = o
            self.rd[k] = []
        dl = []
        for d in deps:
            if d is o:
                continue
            if d.eng == "pe" and eng == "pe" and not d.dma and not dma:
                continue
            d.signal = True
            dl.append(d)
        o.deps = dl
        if isout:
            self.outs.append(o)
        self.ops[eng].append(o)
        return o

    def finalize(self, nc, stack):
        self.esem = {}
        for e in ("pe", "act", "dve", "pool"):
            self.esem[e] = stack.enter_context(nc.semaphore("es_" + e))
        self.dsem = {"sp": [stack.enter_context(nc.semaphore("dsp%d" % i)) for i in range(40)],
                     "pool": [stack.enter_context(nc.semaphore("dpl%d" % i)) for i in range(24)],
                     "act": [stack.enter_context(nc.semaphore("dac%d" % i)) for i in range(4)]}
        fin = Op("sp", lambda e: e.nop(), False)
        fin.deps = list(self.outs)
        self.ops["sp"].append(fin)
        for e in self.ENG:
            cnt = 0
            rr = 0
            tgt = {}
            for o in self.ops[e]:
                if o.dma:
                    pool = self.dsem[e]
                    o.sem = pool[rr % len(pool)]
                    rr += 1
                    o.prev = tgt.get(id(o.sem), 0)
                    o.target = o.prev + 16
                    tgt[id(o.sem)] = o.target
                elif o.signal:
                    cnt += 1
                    o.count = cnt

    def emit(self, eng, e):
        known = {}

        def wait(sem, val):
            k = id(sem)
            if known.get(k, 0) >= val:
                return
            known[k] = val
            e.wait_ge(sem, val)

        for o in self.ops[eng]:
            need = {}
            for d in o.deps:
                if d.dma:
                    s, v = d.sem, d.target
                else:
                    s, v = self.esem[d.eng], d.count
                k = id(s)
                if k not in need or need[k][1] < v:
                    need[k] = (s, v)
            for (s, v) in need.values():
                wait(s, v)
            if o.dma and o.prev > 0:
                wait(o.sem, o.prev)
            ins = o.fn(e)
            if o.dma:
                ins.then_inc(o.sem, 16)
            elif o.signal:
                ins.then_inc(self.esem[eng], 1)


def build_nc():
    nc = bass.Bass("TRN2", target_bir_lowering=False)
    S = Sched()

    def din(name, shape):
        return nc.dram_tensor(name, list(shape), F32, kind="ExternalInput").ap()

    def dout(name, shape):
        return nc.dram_tensor(name, list(shape), F32, kind="ExternalOutput").ap()

    xp_d = din("xp", [TPROMPT, D])
    xs_d = din("xs", [NSEQ * TSAMP, D])
    c17_d = din("c17", [17, D])
    spool_d = din("spool", [NL, NSEQ * 15, D])
    sconv_d = din("sconv", [NL, NSEQ * 3, D])
    slru_d = din("slru", [NL, NSEQ, D])
    w_ada = din("w_ada", [NL, D, 9 * D])
    b_ada = din("b_ada", [NL, 9 * D])
    norm_ffn1 = din("norm_ffn1", [NL, D])
    f1g = din("ffn1_w_gate", [NL, D, DFF])
    f1u = din("ffn1_w_up", [NL, D, DFF])
    f1d = din("ffn1_w_down", [NL, DFF, D])
    norm_mix = din("norm_mix", [NL, D])
    w_in = din("w_in", [NL, D, 7168])
    pool_w = din("pool_w", [NL, 4, 256, 256])
    pool_scale = din("pool_scale", [NL, D])
    conv_w = din("conv_w", [NL, 4, D])
    conv_b = din("conv_b", [NL, D])
    lru_wr = din("lru_wr", [NL, 8, 128, 128])
    lru_br = din("lru_br", [NL, D])
    lru_wi = din("lru_wi", [NL, 8, 128, 128])
    lru_bi = din("lru_bi", [NL, D])
    lru_lambda = din("lru_lambda", [NL, D])
    gmlp_norm = din("gmlp_norm", [NL, 512])
    gmlp_ws = din("gmlp_ws", [NL, 4, 128, 128])
    gmlp_bs = din("gmlp_bs", [NL, 4, 128])
    wbo_pool = din("wbo_pool", [NL, D, D])
    wbo_lru = din("wbo_lru", [NL, D, D])
    wbo_gmlp = din("wbo_gmlp", [NL, 512, D])
    w_out = din("w_out", [NL, D, D])
    norm_ffn2 = din("norm_ffn2", [NL, D])
    f2g = din("ffn2_w_gate", [NL, D, DFF])
    f2u = din("ffn2_w_up", [NL, D, DFF])
    f2d = din("ffn2_w_down", [NL, DFF, D])
    norm_final = din("norm_final", [1, D])

    y_p = dout("y_p", [TPROMPT, D])
    y_s = dout("y_s", [NSEQ * TSAMP, D])
    o_pool_p = dout("o_pool_p", [NL, 15, D])
    o_conv_p = dout("o_conv_p", [NL, 3, D])
    o_lru_p = dout("o_lru_p", [NL, 1, D])
    o_pool_s = dout("o_pool_s", [NL, NSEQ * 15, D])
    o_conv_s = dout("o_conv_s", [NL, NSEQ * 3, D])
    o_lru_s = dout("o_lru_s", [NL, NSEQ, D])
    o_v_s = dout("o_v_s", [NL, NSEQ * TSAMP, 512])

    stack = ExitStack()

    def sb(name, shape, dt=F32):
        return stack.enter_context(nc.sbuf_tensor(name, list(shape), dt))

    xT = sb("xT", [P, KC, TMAX])
    hT = sb("hT", [P, KC, TMAX], BF16)
    ring = sb("ring", [P, NSLOT, SLOT_EL], BF16)
    modT = sb("modT", [P, NL, 72, 17])
    vecT = sb("vecT", [P, V_BLK * 128])
    ident = sb("ident", [P, P])
    onesf = sb("onesf", [P, P])
    onesb = sb("onesb", [P, P], BF16)
    wmT = sb("wmT", [P, 16, P], BF16)
    BD = sb("BD", [P, 16, P], BF16)
    bsb = sb("bsb", [P, 2, 4, P])
    gnb = sb("gnb", [P, 512])
    scT = sb("scT", [P, KC, 17], BF16)
    poolst = sb("poolst", [P, NL, KC, 15])
    convst = sb("convst", [P, NL, KC, 3])
    lrust = sb("lrust", [P, NL, KC])
    lruc = sb("lruc", [P, NL, 4, KC])
    rcnt = sb("rcnt", [P, 16])
    stg = sb("stg", [P, D])
    merged = sb("merged", [P, KC, TMAX], BF16)
    ybr = sb("ybr", [P, KC, TMAX], BF16)
    SCRW = 10240
    scr = sb("scr", [P, SCRW])
    psum = stack.enter_context(nc.psum_tensor("psum", [P, 8, 512], F32))

    def scrv(off_words, shape, dt=F32):
        n = int(np.prod(shape))
        words = n if dt == F32 else (n + 1) // 2
        assert off_words + words <= SCRW, (off_words, shape)
        v = scr[:, off_words:off_words + words]
        if dt != F32:
            v = v.bitcast(dt)
        if len(shape) == 1:
            return v
        names = " ".join("d%d" % i for i in range(len(shape)))
        kw = {"d%d" % i: shape[i] for i in range(1, len(shape))}
        return v.rearrange("p (%s) -> p %s" % (names, names), **kw)

    psrot = {"mm": [0, 1, 2, 3], "aux": [4, 5], "tr": [6, 7]}
    psidx = {"mm": 0, "aux": 0, "tr": 0}

    def ps(pool="mm"):
        b = psrot[pool][psidx[pool] % len(psrot[pool])]
        psidx[pool] += 1
        return psum[:, b, :]

    class WV:
        def __init__(self, ap, slot, gen):
            self.ap, self.slot, self.gen = ap, slot, gen

        def __getitem__(self, idx):
            return WV(self.ap[idx], self.slot, self.gen)

    ring_gen = [0] * NSLOT

    def unwrap(x):
        if isinstance(x, WV):
            assert ring_gen[x.slot] == x.gen, "weight ring slot reused before its consumer was recorded"
            return x.ap
        return x

    def mm(out, lhsT, rhs, start, stop):
        lhsT = unwrap(lhsT)
        rhs = unwrap(rhs)
        S.add("pe", lambda e: e.matmul(out, lhsT=lhsT, rhs=rhs, start=start, stop=stop),
              reads=[lhsT, rhs], writes=[out])

    def tr(out, in_):
        k = in_.shape[0]
        idn = ident[0:k, 0:k]
        S.add("pe", lambda e: e.transpose(out, in_, idn), reads=[in_, idn], writes=[out])

    def act(out, in_, func, bias=None, scale=None, accum_out=None, extra_r=(), extra_w=()):
        kw = {}
        if bias is not None:
            kw["bias"] = bias
        if scale is not None:
            kw["scale"] = scale
        if accum_out is not None:
            kw["accum_out"] = accum_out
        rd = [in_] + [a for a in (bias, scale) if a is not None and not isinstance(a, (int, float))] + list(extra_r)
        wr = [out] + ([accum_out] if accum_out is not None else []) + list(extra_w)
        S.add("act", lambda e: e.activation(out=out, in_=in_, func=func, **kw), reads=rd, writes=wr)

    def tt(out, in0, in1, op, eng="dve"):
        S.add(eng, lambda e: e.tensor_tensor(out=out, in0=in0, in1=in1, op=op), reads=[in0, in1], writes=[out])

    def ts(out, in0, s1, s2, op0, op1=None, eng="dve"):
        rd = [in0] + [a for a in (s1, s2) if a is not None and not isinstance(a, (int, float))]
        if op1 is None:
            S.add(eng, lambda e: e.tensor_scalar(out=out, in0=in0, scalar1=s1, scalar2=None, op0=op0),
                  reads=rd, writes=[out])
        else:
            S.add(eng, lambda e: e.tensor_scalar(out=out, in0=in0, scalar1=s1, scalar2=s2, op0=op0, op1=op1),
                  reads=rd, writes=[out])

    def stt(out, in0, scalar, in1, op0, op1, eng="dve"):
        rd = [in0, in1] + ([scalar] if not isinstance(scalar, (int, float)) else [])
        S.add(eng, lambda e: e.scalar_tensor_tensor(out=out, in0=in0, scalar=scalar, in1=in1, op0=op0, op1=op1),
              reads=rd, writes=[out])

    def cp(out, in_, eng="dve"):
        S.add(eng, lambda e: e.tensor_copy(out=out, in_=in_), reads=[in_], writes=[out])

    def memset(ap, val, eng="pool", wkeys=None):
        S.add(eng, lambda e: e.memset(ap, val), writes=[ap] if wkeys is None else wkeys)

    def recip(out, in_):
        S.add("dve", lambda e: e.reciprocal(out=out, in_=in_), reads=[in_], writes=[out])

    def scan(out, d0, d1, init):
        rd = [d0, d1] + ([init] if not isinstance(init, (int, float)) else [])
        S.add("dve", lambda e: e.tensor_tensor_scan(out=out, data0=d0, data1=d1, initial=init,
                                                    op0=ALU.mult, op1=ALU.add), reads=rd, writes=[out])

    def dma(out, in_, q="sp", reads=None, writes=None, slow=False):
        if slow:
            fn = lambda e: e.dma_start(out=out, in_=in_, allow_slow_non_contiguous=True)
        else:
            fn = lambda e: e.dma_start(out=out, in_=in_)
        S.add(q, fn, reads=[in_] if reads is None else reads, writes=[out] if writes is None else writes, dma=True)

    ringi = [0]
    ada_hook = [None]

    def wload(parts):
        s = ringi[0] % NSLOT
        ringi[0] += 1
        ring_gen[s] += 1
        views = []
        for (off, shape, src) in parts:
            n = int(np.prod(shape))
            assert off + n <= SLOT_EL
            v = ring[:, s, off:off + n]
            if len(shape) == 2:
                v = v.rearrange("p (a b) -> p a b", a=shape[0], b=shape[1])
            dma(v, src, q="pool")
            views.append(WV(v, s, ring_gen[s]))
        if ada_hook[0] is not None:
            ada_hook[0]()
        return views

    def wcols(W, c0, n):
        return W.rearrange("(k p) n -> p k n", p=P)[:, :, c0:c0 + n]

    def vcol(l, name, k):
        r = l * VPL + VR[name] + k
        return vecT[:, r:r + 1]

    def vcols(l, name, k0, n):
        r = l * VPL + VR[name] + k0
        return vecT[:, r:r + n]

    memset(onesf[:], 1.0)
    S.add("pool", lambda e: e.affine_select(ident[:], onesf[:], pattern=[[-1, P]], compare_op=ALU.is_equal,
                                            fill=0.0, base=0, channel_multiplier=1),
          reads=[onesf[:]], writes=[ident[:]])
    memset(onesb[:], 1.0 / D)
    memset(poolst[:], 0.0)
    memset(convst[:], 0.0)
    memset(lrust[:], 0.0)
    for t in range(16):
        memset(rcnt[:, t:t + 1], 1.0 / (t + 1))
    vstage = scrv(0, [V_BLK, P])
    vkeys = []
    memset(vstage, 0.0)
    vms = S.ops["pool"][-1]

    def vload(row0, src2d):
        n = src2d.shape[0]
        r = row0
        done = 0
        while done < n:
            blk, p0 = divmod(r, 128)
            m = min(n - done, 128 - p0)
            key = ("vst", len(vkeys))
            vkeys.append(key)
            dma(vstage[p0:p0 + m, blk, :], src2d[done:done + m, :], writes=[key])
            o = S.ops["sp"][-1]
            o.deps = list(o.deps) + [vms]
            vms.signal = True
            done += m
            r += m

    for l in range(NL):
        b0 = l * VPL
        vload(b0 + VR["b_ada"], b_ada[l].rearrange("(r c) -> r c", c=P))
        vload(b0 + VR["n1"], norm_ffn1[l].rearrange("(r c) -> r c", c=P))
        vload(b0 + VR["nm"], norm_mix[l].rearrange("(r c) -> r c", c=P))
        vload(b0 + VR["n2"], norm_ffn2[l].rearrange("(r c) -> r c", c=P))
        vload(b0 + VR["pscale"], pool_scale[l].rearrange("(r c) -> r c", c=P))
        vload(b0 + VR["convw"], conv_w[l].rearrange("k (r c) -> (k r) c", c=P))
        vload(b0 + VR["convb"], conv_b[l].rearrange("(r c) -> r c", c=P))
        vload(b0 + VR["br"], lru_br[l].rearrange("(r c) -> r c", c=P))
        vload(b0 + VR["bi"], lru_bi[l].rearrange("(r c) -> r c", c=P))
        vload(b0 + VR["lam"], lru_lambda[l].rearrange("(r c) -> r c", c=P))
    vload(V_NF, norm_final[0].rearrange("(r c) -> r c", c=P))
    for blk in range(V_BLK):
        pt = ps("tr")
        k_ = vstage[:, blk, :].shape[0]
        S.add("pe", (lambda pt=pt, blk=blk: lambda e: e.transpose(pt[:, 0:P], vstage[:, blk, :], ident[:]))(),
              reads=[vstage[:, blk, :], ident[:]] + vkeys, writes=[pt[:, 0:P]])
        act(vecT[:, blk * P:(blk + 1) * P], pt[:, 0:P], AF.Copy)
    for l in range(NL):
        lamv = vcols(l, "lam", 0, KC)
        e1 = scrv(4096, [KC])
        act(e1, lamv, AF.Exp, scale=-1.0)
        act(e1, e1, AF.Ln, bias=1.0)
        ts(lruc[:, l, 0, :], e1, -4.0, None, ALU.mult)
        ts(lruc[:, l, 1, :], e1, -8.0, None, ALU.mult)
        ts(lruc[:, l, 2, :], vcols(l, "br", 0, KC), 0.5, None, ALU.mult)
        ts(lruc[:, l, 3, :], vcols(l, "bi", 0, KC), 0.5, None, ALU.mult)
    c17s = scrv(0, [D])
    dma(c17s[0:17, :], c17_d, writes=[c17s[0:17, :]] + vkeys)
    ptc = ps("tr")
    for k in range(KC):
        tr(ptc[:, k * 17:(k + 1) * 17], c17s[0:17, k * P:(k + 1) * P])
    act(scT[:], ptc[:, 0:KC * 17].rearrange("p (k c) -> p k c", c=17), AF.Silu)
    wsst = scrv(0, [16, P])
    dma(wsst, gmlp_ws.rearrange("l g t s -> t (l g) s"))
    for lg in range(16):
        S.add("pool", (lambda lg: lambda e: e.affine_select(wsst[:, lg, :], wsst[:, lg, :], pattern=[[-1, P]],
                                                             compare_op=ALU.is_ge, fill=0.0, base=0,
                                                             channel_multiplier=1))(lg),
              reads=[wsst[:, lg, :]], writes=[wsst[:, lg, :]])
        pt = ps("tr")
        tr(pt[:, 0:P], wsst[:, lg, :])
        act(wmT[:, lg, :], pt[:, 0:P], AF.Copy)

    def ada_slot(l, s):
        (wv,) = wload([(0, [KC, 512], wcols(w_ada[l], s * 512, 512))])
        pa = ps("tr")
        for m4 in range(4):
            for k in range(KC):
                mm(pa[:, m4 * 17:(m4 + 1) * 17], wv[:, k, m4 * P:(m4 + 1) * P], scT[:, k, :], k == 0, k == KC - 1)
        for m4 in range(4):
            m = s * 4 + m4
            act(modT[:, l, m, :], pa[:, m4 * 17:(m4 + 1) * 17], AF.Identity, bias=vcol(l, "b_ada", m), scale=1.0)

    def ada_post(l, whs):
        for (wh, nm) in ((1, "n1"), (4, "nm"), (7, "n2")):
            if wh not in whs:
                continue
            sl = modT[:, l, wh * 8:(wh + 1) * 8, :]
            nb = vcols(l, nm, 0, KC).unsqueeze(2).to_broadcast([P, KC, 17])
            stt(sl, sl, 1.0, nb, ALU.add, ALU.mult)
        for wh in (2, 5, 8):
            if wh not in whs:
                continue
            sl = modT[:, l, wh * 8:(wh + 1) * 8, :]
            ts(sl, sl, 0.5, None, ALU.mult)

    ada_q = []
    ada_up = [(lambda s_=s_: ada_slot(0, s_)) for s_ in range(6)] + [lambda: ada_post(0, (1, 2))]
    for s_ in range(6, 10):
        ada_q.append((0, 0, (lambda s_=s_: ada_slot(0, s_))))
    ada_q.append((0, 0, (lambda: ada_post(0, (4,)))))
    for s_ in range(10, 18):
        ada_q.append((0, 1, (lambda s_=s_: ada_slot(0, s_))))
    ada_q.append((0, 1, (lambda: ada_post(0, (5, 7, 8)))))
    for l_ in range(1, NL):
        for s_ in range(18):
            ada_q.append((l_, 0, (lambda l_=l_, s_=s_: ada_slot(l_, s_))))
        ada_q.append((l_, 0, (lambda l_=l_: ada_post(l_, (1, 2, 4, 5, 7, 8)))))
    ada_ctl = {"on": False, "lmax": 0, "tick": 0, "busy": False, "grpA": False}

    def ada_tick():
        if not ada_ctl["on"] or ada_ctl["busy"] or not ada_q:
            return
        l_, st, fn = ada_q[0]
        if l_ > ada_ctl["lmax"]:
            return
        ada_q.pop(0)
        ada_ctl["busy"] = True
        fn()
        ada_ctl["busy"] = False

    def ada_drain(l, st):
        while ada_q and (ada_q[0][0], ada_q[0][1]) <= (l, st):
            fn = ada_q.pop(0)[2]
            ada_ctl["busy"] = True
            fn()
            ada_ctl["busy"] = False

    ada_hook[0] = ada_tick

    def modp(l, wh, k):
        return modT[:, l, wh * 8 + k, 0:1]

    def mods(l, wh, k0=0, nk=KC):
        return modT[:, l, wh * 8 + k0:wh * 8 + k0 + nk, 1:17].unsqueeze(3).to_broadcast([P, nk, NSEQ, TSAMP])

    bdkeys = [("BD", lg, j) for lg in range(16) for j in range(16)]
    memset(BD[:], 0.0, wkeys=bdkeys)

    def build_bd():
        for lg in range(16):
            for j in range(16):
                dma(BD[8 * j:8 * j + 8, lg, 8 * j:8 * j + 8], wmT[0:8, lg, 0:8],
                    reads=[wmT[:, lg, :]], writes=[("BD", lg, j)])

    def load_inputs(grp):
        xst = scrv(0, [4, D])
        for t in grp:
            if t.kind == "p":
                dma(xst, xp_d[t.pos0:t.pos0 + 512, :].rearrange("(s p) d -> p s d", p=P))
                nsub = 4
            else:
                dma(xst[:, 0, :], xs_d)
                nsub = 1
            for k in range(KC):
                pt = ps("tr")
                for s in range(nsub):
                    tr(pt[:, s * P:(s + 1) * P], xst[:, s, k * P:(k + 1) * P])
                act(xT[:, k, t.col:t.col + t.n], pt[:, 0:t.n], AF.Copy)

    nrm_i = [0]

    def rstd_stage1(t):
        n = t.n
        par = nrm_i[0] % 2
        nrm_i[0] += 1
        sq = scrv(par * 2048, [KC, 512], BF16)
        rstd = scrv(4096 + par * 512, [512])
        act(sq[:, :, 0:n], xT[:, :, t.col:t.col + n], AF.Square)

        def stage2():
            pb = ps("aux")
            for k in range(KC):
                mm(pb[:, 0:n], onesb[:], sq[:, k, 0:n], k == 0, k == KC - 1)
            act(rstd[:, 0:n], pb[:, 0:n], AF.Ln, bias=EPS, scale=1.0)
            act(rstd[:, 0:n], rstd[:, 0:n], AF.Exp, scale=-0.5)
            return rstd[:, 0:n]
        return stage2

    def rstd_tile(t):
        return rstd_stage1(t)()

    def norm_apply(l, which, t, rstd):
        wsh, wA = which * 3, which * 3 + 1
        n = t.n
        if t.kind == "p":
            for k in range(KC):
                tmp = scrv(5120 + (k % 4) * 512, [512])
                tt(tmp[:, 0:n], xT[:, k, t.col:t.col + n], rstd, ALU.mult)
                act(hT[:, k, t.col:t.col + n], tmp[:, 0:n], AF.Identity, bias=modp(l, wsh, k), scale=modp(l, wA, k))
        else:
            tmp = scrv(5120, [KC, P])
            tt(tmp, xT[:, :, t.col:t.col + n], rstd.unsqueeze(1).to_broadcast([P, KC, n]), ALU.mult)
            t4 = tmp.rearrange("p k (s t) -> p k s t", t=TSAMP)
            tt(t4, t4, mods(l, wA), ALU.mult)
            tt(hT[:, :, t.col:t.col + n].rearrange("p k (s t) -> p k s t", t=TSAMP), t4, mods(l, wsh), ALU.add)

    def norm_stage1(l, which, t):
        s2 = rstd_stage1(t)
        return lambda: norm_apply(l, which, t, s2())

    def norm_mod(l, which, grp):
        for t in grp:
            norm_stage1(l, which, t)()

    def tile_outer(grp, body, after1):
        pend = None
        for t in grp:
            for m in range(KC):
                body(t, m)
                if m == 1 and pend is not None:
                    pend()
                    pend = None
            pend = after1(t) if after1 is not None else None
        if pend is not None:
            pend()

    def resid_update(l, wh, m, t, pb):
        n = t.n
        xs_ = xT[:, m, t.col:t.col + n]
        if t.kind == "p":
            stt(xs_, pb[:, 0:n], modp(l, wh, m), xs_, ALU.mult, ALU.add)
        else:
            tmp = scrv(9984, [P])
            g3 = modT[:, l, wh * 8 + m, 1:17].unsqueeze(2).to_broadcast([P, NSEQ, TSAMP])
            tt(tmp.rearrange("p (s t) -> p s t", t=TSAMP), pb[:, 0:n].rearrange("p (s t) -> p s t", t=TSAMP), g3, ALU.mult)
            tt(xs_, xs_, tmp, ALU.add)

    def ffn(l, which, grp, ada_l=None, after1=None):
        ada_ctl["on"] = ada_ctl["grpA"]
        ffn_(l, which, grp, after1)
        ada_ctl["on"] = False

    def ffn_(l, which, grp, after1):
        Wg, Wu, Wd = ((f1g, f1u, f1d), (f2g, f2u, f2d))[which]
        wh_gt = 2 if which == 0 else 8
        actb = merged
        for half in range(2):
            j0 = half * HALF_FF
            for jj in range(0, HALF_FF, 2):
                nj = min(2, HALF_FF - jj)
                c0 = (j0 + jj) * P
                wg, wu = wload([(0, [KC, nj * P], wcols(Wg[l], c0, nj * P)),
                                (2048, [KC, nj * P], wcols(Wu[l], c0, nj * P))])
                order = [(ji, t) for ji in range(nj) for t in grp]
                if half == 0 and jj == 0:
                    order = [(ji, t) for t in grp for ji in range(nj)]
                for (ji, t) in order:
                    jl = jj + ji
                    if True:
                        n = t.n
                        rhs_cols = slice(t.col, t.col + n)
                        pg = ps("mm")
                        for k in range(KC):
                            mm(pg[:, 0:n], wg[:, k, ji * P:(ji + 1) * P], hT[:, k, rhs_cols], k == 0, k == KC - 1)
                        pu = ps("mm")
                        for k in range(KC):
                            mm(pu[:, 0:n], wu[:, k, ji * P:(ji + 1) * P], hT[:, k, rhs_cols], k == 0, k == KC - 1)
                        sg = scrv((ffsg[0] % 4) * 512, [512])
                        ffsg[0] += 1
                        act(sg[:, 0:n], pg[:, 0:n], AF.Silu)
                        tt(ffact(jl)[:, rhs_cols], sg[:, 0:n], pu[:, 0:n], ALU.mult)
            if half == 0:
                for m2 in range(0, KC, 2):
                    (wd,) = wload([(0, [HALF_FF, 2 * P],
                                    Wd[l].rearrange("(j p) n -> p j n", p=P)[:, j0:j0 + HALF_FF, m2 * P:(m2 + 2) * P])])
                    for mi in range(2):
                        m = m2 + mi
                        for t in grp:
                            n = t.n
                            py = ps("mm")
                            for jl in range(HALF_FF):
                                mm(py[:, 0:n], wd[:, jl, mi * P:(mi + 1) * P], ffact(jl)[:, t.col:t.col + n],
                                   jl == 0, jl == HALF_FF - 1)
                            resid_update(l, wh_gt, m, t, py)
            else:
                wds = []
                hold = ada_ctl["busy"]
                ada_ctl["busy"] = True
                for m2 in range(0, KC, 2):
                    (wd,) = wload([(0, [HALF_FF, 2 * P],
                                    Wd[l].rearrange("(j p) n -> p j n", p=P)[:, j0:j0 + HALF_FF, m2 * P:(m2 + 2) * P])])
                    wds.append(wd)
                ada_ctl["busy"] = hold

                def body(t, m):
                    n = t.n
                    wd = wds[m // 2]
                    mi = m % 2
                    py = ps("mm")
                    for jl in range(HALF_FF):
                        mm(py[:, 0:n], wd[:, jl, mi * P:(mi + 1) * P], ffact(jl)[:, t.col:t.col + n],
                           jl == 0, jl == HALF_FF - 1)
                    resid_update(l, wh_gt, m, t, py)
                if which == 0:
                    ada_drain(l, 0)
                else:
                    ada_drain(l + 1, 1)
                tile_outer(grp, body, after1)

    ffsg = [0]

    def ffact(jl):
        if jl < 8:
            return merged[:, jl, :]
        return ybr[:, jl - 8, :]

    gmi = [0]

    def gate_merge(l, grp, br, Wbo, nk):
        for s4 in range(2):
            (wb,) = wload([(0, [nk, 512], wcols(Wbo[l], s4 * 512, 512))])
            (wgt,) = wload([(0, [KC, 512], wcols(w_in[l], O_G0 + br * D + s4 * 512, 512))])
            for m4 in range(4):
                m = s4 * 4 + m4
                for t in grp:
                    n = t.n
                    cols = slice(t.col, t.col + n)
                    pp = ps("mm")
                    for k in range(nk):
                        mm(pp[:, 0:n], wb[:, k, m4 * P:(m4 + 1) * P], ybr[:, k, cols], k == 0, k == nk - 1)
                    pgt = ps("aux")
                    for k in range(KC):
                        mm(pgt[:, 0:n], wgt[:, k, m4 * P:(m4 + 1) * P], hT[:, k, cols], k == 0, k == KC - 1)
                    i = gmi[0]
                    gmi[0] += 1
                    sg = scrv(8192 + (i % 2) * 512, [512])
                    act(sg[:, 0:n], pgt[:, 0:n], AF.Tanh, scale=0.5)
                    if br == 0:
                        stt(merged[:, m, cols], sg[:, 0:n], 1.0, pp[:, 0:n], ALU.add, ALU.mult)
                    else:
                        tmp = scrv(9216 + (i % 2) * 512, [512])
                        stt(tmp[:, 0:n], sg[:, 0:n], 1.0, pp[:, 0:n], ALU.add, ALU.mult)
                        tt(merged[:, m, cols], merged[:, m, cols], tmp[:, 0:n], ALU.add)

    def store_rows(src_fn, ncols, dram_rows, ):
        for kb in range(2):
            pt = ps("tr")
            for k4 in range(4):
                tr(pt[0:ncols, k4 * P:(k4 + 1) * P], src_fn(kb * 4 + k4))
            act(stg[0:ncols, kb * 512:(kb + 1) * 512], pt[0:ncols, :], AF.Copy)
        dma(dram_rows, stg[0:ncols, :])

    def load_rows_T(dram_rows, nrows, dst_fn):
        dma(stg[0:nrows, :], dram_rows)
        for k in range(KC):
            pt = ps("tr")
            tr(pt[:, 0:nrows], stg[0:nrows, k * P:(k + 1) * P])
            act(dst_fn(k), pt[:, 0:nrows], AF.Copy)

    def mixer(l, grp, gi, after1=None):
        has_s = any(t.kind == "s" for t in grp)
        last_prompt = (gi == len(GROUPS) - 1)
        dma(bsb[:, 0], gmlp_bs[l].partition_broadcast(P))
        for g_ in range(4):
            dma(bsb[:, 1, g_, :].rearrange("p (j t) -> p j t", t=TSAMP),
                gmlp_bs[l, g_, 0:TSAMP].partition_broadcast(P).unsqueeze(1).to_broadcast([P, NSEQ, TSAMP]), slow=True)
        dma(gnb[:], gmlp_norm[l].partition_broadcast(P))
        spT = scrv(0, [KC, NSEQ * 15])
        scvT = scrv(1920, [KC, NSEQ * 3])
        h0T = scrv(2304, [KC, NSEQ])
        hTs = scrv(2432, [KC, NSEQ])
        BASE = 2560
        def load_pool_state():
            for hh in range(2):
                load_rows_T(spool_d[l, hh * 120:(hh + 1) * 120, :], 120,
                            lambda k, hh=hh: spT[:, k, hh * 120:(hh + 1) * 120])

        def load_lru_state():
            load_rows_T(sconv_d[l], 48, lambda k: scvT[:, k, :])
            load_rows_T(slru_d[l], 16, lambda k: h0T[:, k, :])

        def pool_A(g, t, par, wx, wp):
            w = 2 << g
            n, nseq, tl = t.n, t.nseq, t.tl
            L = 15 + tl
            U = BASE + par * 3712
            xcat = scrv(U, [2, nseq, L])
            wbufs = [scrv(U + 1056, [2, nseq, L]), scrv(U + 2112, [2, nseq, L])]
            dbuf = scrv(U + 3168, [2, nseq, tl], BF16)
            for c in range(2):
                ch = g * 2 + c
                if t.kind == "p":
                    cp(xcat[:, c, 0, 0:15], poolst[:, l, ch, :])
                else:
                    cp(xcat[:, c, :, 0:15], spT[:, ch, :].rearrange("p (s r) -> p s r", r=15))
                pb = ps("mm")
                for k in range(KC):
                    mm(pb[:, 0:n], wx[:, k, c * P:(c + 1) * P], hT[:, k, t.col:t.col + n], k == 0, k == KC - 1)
                act(xcat[:, c, :, 15:L], pb[:, 0:n].rearrange("p (s t) -> p s t", t=tl), AF.Copy)
                if t.kind == "p":
                    cp(poolst[:, l, ch, :], xcat[:, c, 0, L - 15:L])
                else:
                    cp(spT[:, ch, :].rearrange("p (s r) -> p s r", r=15), xcat[:, c, :, L - 15:L])
            cur, curlen = xcat, L
            for i in range(g + 1):
                sh = 1 << i
                o = wbufs[i % 2]
                tt(o[:, :, :, 0:curlen - sh], cur[:, :, :, sh:curlen], cur[:, :, :, 0:curlen - sh], ALU.add)
                cur, curlen = o, curlen - sh
            off = 16 - w
            stt(dbuf[:], cur[:, :, :, off:off + tl], 1.0 / w, xcat[:, :, :, 15:L], ALU.mult, ALU.subtract)
            if t.kind == "p" and t.pos0 == 0:
                nf = w - 1
                fx = scrv(U + 3680, [2, 1, 16])
                tt(fx[:, :, :, 0:nf], cur[:, :, :, off:off + nf],
                   rcnt[:, 0:nf].unsqueeze(1).unsqueeze(1).to_broadcast([P, 2, 1, nf]), ALU.mult)
                tt(dbuf[:, :, :, 0:nf], fx[:, :, :, 0:nf], xcat[:, :, :, 15:15 + nf], ALU.subtract)

            def stage_b():
                for m in range(2):
                    pb2 = ps("aux")
                    for k in range(2):
                        mm(pb2[:, 0:n], wp[:, k, m * P:(m + 1) * P],
                           dbuf[:, k].rearrange("p s t -> p (s t)"), k == 0, k == 1)
                    act(ybr[:, g * 2 + m, t.col:t.col + n], pb2[:, 0:n], AF.Identity,
                        scale=vcol(l, "pscale", g * 2 + m), bias=0.0)
            return stage_b

        pend = None
        ui = 0
        for gp in range(2):
            wts = {}
            hold = ada_ctl["busy"]
            ada_ctl["busy"] = True
            for g in (2 * gp, 2 * gp + 1):
                wts[g] = wload([(0, [KC, 256], wcols(w_in[l], O_XP + g * 256, 256)),
                                (2048, [2, 256], pool_w[l, g].rearrange("(k p) n -> p k n", p=P))])
            ada_ctl["busy"] = hold
            for t in grp:
                if t.kind == "s" and gp == 0:
                    load_pool_state()
                for g in (2 * gp, 2 * gp + 1):
                    wx, wp = wts[g]
                    nb = pool_A(g, t, ui % 2, wx, wp)
                    ui += 1
                    if pend is not None:
                        pend()
                    pend = nb
        pend()
        if has_s:
            for hh in range(2):
                store_rows(lambda k, hh=hh: spT[:, k, hh * 120:(hh + 1) * 120], 120,
                           o_pool_s[l, hh * 120:(hh + 1) * 120, :])
        if last_prompt:
            store_rows(lambda k: poolst[:, l, k, :], 15, o_pool_p[l])
        gate_merge(l, grp, 0, wbo_pool, KC)

        def lru_A(ch, t, par, wxl, wgl, wr_, wi_):
            n, nseq, tl = t.n, t.nseq, t.tl
            L = 3 + tl
            U = BASE + par * 3344
            xlc = scrv(U, [nseq, L])
            xc = scrv(U + 528, [nseq, tl])
            xcb = scrv(U + 1040, [512], BF16)
            trb = scrv(U + 1296, [512])
            tib = scrv(U + 1808, [512])
            aa = scrv(U + 2320, [512])
            sq = scrv(U + 2832, [512])
            gel = scrv(par * 512, [512])
            t16 = scrv(1024 + par * 16, [NSEQ])
            cols = slice(t.col, t.col + n)
            if t.kind == "p":
                cp(xlc[:, 0, 0:3], convst[:, l, ch, :])
            else:
                cp(xlc[:, :, 0:3], scvT[:, ch, :].rearrange("p (s r) -> p s r", r=3))
            pb = ps("mm")
            for k in range(KC):
                mm(pb[:, 0:n], wxl[:, k, :], hT[:, k, cols], k == 0, k == KC - 1)
            cp(xlc[:, :, 3:L], pb[:, 0:n].rearrange("p (s t) -> p s t", t=tl))
            if t.kind == "p":
                cp(convst[:, l, ch, :], xlc[:, 0, L - 3:L])
            else:
                cp(scvT[:, ch, :].rearrange("p (s r) -> p s r", r=3), xlc[:, :, L - 3:L])
            cw = lambda kk: vcol(l, "convw", kk * 8 + ch)
            ts(xc[:], xlc[:, :, 0:tl], cw(0), vcol(l, "convb", ch), ALU.mult, ALU.add)
            for kk in range(1, 4):
                stt(xc[:], xlc[:, :, kk:kk + tl], cw(kk), xc[:], ALU.mult, ALU.add)
            xcf = xc.rearrange("p s t -> p (s t)")
            cp(xcb[:, 0:n], xcf)
            pg = ps("mm")
            for k in range(KC):
                mm(pg[:, 0:n], wgl[:, k, :], hT[:, k, cols], k == 0, k == KC - 1)

            def stage_b():
                pr = ps("aux")
                mm(pr[:, 0:n], wr_, xcb[:, 0:n], True, True)
                pi = ps("aux")
                mm(pi[:, 0:n], wi_, xcb[:, 0:n], True, True)
                act(gel[:, 0:n], pg[:, 0:n], AF.Gelu_apprx_tanh)
                act(trb[:, 0:n], pr[:, 0:n], AF.Tanh, bias=lruc[:, l, 2, ch:ch + 1], scale=0.5)
                act(tib[:, 0:n], pi[:, 0:n], AF.Tanh, bias=lruc[:, l, 3, ch:ch + 1], scale=0.5)
                act(aa[:, 0:n], trb[:, 0:n], AF.Exp, scale=lruc[:, l, 0, ch:ch + 1], bias=lruc[:, l, 0, ch:ch + 1])
                act(sq[:, 0:n], trb[:, 0:n], AF.Exp, scale=lruc[:, l, 1, ch:ch + 1], bias=lruc[:, l, 1, ch:ch + 1])
                act(sq[:, 0:n], sq[:, 0:n], AF.Ln, scale=-1.0, bias=1.0)
                act(sq[:, 0:n], sq[:, 0:n], AF.Exp, scale=0.5)
                stt(tib[:, 0:n], tib[:, 0:n], 1.0, xcf, ALU.add, ALU.mult)
                stt(tib[:, 0:n], tib[:, 0:n], 0.5, sq[:, 0:n], ALU.mult, ALU.mult)
                if t.kind == "p":
                    scan(xcf, aa[:, 0:n], tib[:, 0:n], lrust[:, l, ch:ch + 1])
                    cp(lrust[:, l, ch:ch + 1], xcf[:, n - 1:n])
                else:
                    tt(t16[:], aa[:, 0:n:TSAMP], h0T[:, ch, :], ALU.mult)
                    tt(tib[:, 0:n:TSAMP], tib[:, 0:n:TSAMP], t16[:], ALU.add)
                    memset(aa[:, 0:n:TSAMP], 0.0, eng="dve")
                    scan(xcf, aa[:, 0:n], tib[:, 0:n], 0.0)
                    cp(hTs[:, ch, :], xcf[:, TSAMP - 1:n:TSAMP])
                tt(ybr[:, ch, cols], xcf, gel[:, 0:n], ALU.mult)
            return stage_b

        pend = None
        ui = 0
        for ch in range(KC):
            wxl, wgl, wr_, wi_ = wload([(0, [KC, P], wcols(w_in[l], O_XL + ch * P, P)),
                                        (1024, [KC, P], wcols(w_in[l], O_GL + ch * P, P)),
                                        (2048, [P], lru_wr[l, ch]),
                                        (2176, [P], lru_wi[l, ch])])
            for t in grp:
                if t.kind == "s" and ch == 0:
                    load_lru_state()
                nb = lru_A(ch, t, ui % 2, wxl, wgl, wr_, wi_)
                ui += 1
                if pend is not None:
                    pend()
                pend = nb
        pend()
        if has_s:
            store_rows(lambda k: scvT[:, k, :], 48, o_conv_s[l])
            store_rows(lambda k: hTs[:, k, :], 16, o_lru_s[l])
        if last_prompt:
            store_rows(lambda k: convst[:, l, k, :], 3, o_conv_p[l])
            store_rows(lambda k: lrust[:, l, k:k + 1], 1, o_lru_p[l])
        gate_merge(l, grp, 1, wbo_lru, KC)

        (wv_,) = wload([(0, [KC, 512], wcols(w_in[l], O_V, 512))])
        sall = scrv(BASE, [4, TMAX])
        vjunk = scrv(BASE + 4608, [512], BF16)

        def gm_A(t, sub, par):
            c0 = t.col + sub * P
            vnb = scrv(BASE + 4864 + par * 256, [512], BF16)
            vnf = scrv(1024, [512])
            ssv = scrv(1536 + par * 4, [4])
            pv = ps("mm")
            for k in range(KC):
                mm(pv[:, :], hT[:, k, c0:c0 + P], wv_[:, k, :], k == 0, k == KC - 1)
            act(vjunk[:], pv[:, :], AF.Square, accum_out=ssv[:, 0:1])
            act(ssv[:, 1:2], ssv[:, 0:1], AF.Ln, scale=1.0 / 512, bias=EPS)
            act(ssv[:, 1:2], ssv[:, 1:2], AF.Exp, scale=-0.5)
            if t.kind == "p":
                stt(vnb[:], pv[:, :], ssv[:, 1:2], gnb[:], ALU.mult, ALU.mult)
            else:
                stt(vnf[:], pv[:, :], ssv[:, 1:2], gnb[:], ALU.mult, ALU.mult)
                act(vnb[:], vnf[:], AF.Copy)
                dma(o_v_s[l], vnf[:])

            def stage_b():
                psx = ps("aux")
                for g in range(4):
                    lg = l * 4 + g
                    o_ = psx[:, g * P:(g + 1) * P]
                    l_ = vnb[:, g * P:(g + 1) * P]
                    if t.kind == "p":
                        mm(o_, l_, wmT[:, lg, :], True, True)
                    else:
                        b_ = BD[:, lg, :]
                        S.add("pe", (lambda o_=o_, l_=l_, b_=b_: lambda e: e.matmul(o_, lhsT=l_, rhs=b_, start=True, stop=True))(),
                              reads=[l_] + [("BD", lg, j) for j in range(16)], writes=[o_])
                tt(sall[:, :, c0:c0 + P], psx[:, :].rearrange("p (g t) -> p g t", t=P),
                   bsb[:, 0 if t.kind == "p" else 1], ALU.add)
            return stage_b

        pend = None
        si = 0
        for t in grp:
            for sub in range(t.n // P):
                nb = gm_A(t, sub, si % 2)
                si += 1
                if pend is not None:
                    pend()
                pend = nb
        pend()
        (wu_,) = wload([(0, [KC, 512], wcols(w_in[l], O_U, 512))])
        for g in range(4):
            for t in grp:
                n = t.n
                cols = slice(t.col, t.col + n)
                pu = ps("mm")
                for k in range(KC):
                    mm(pu[:, 0:n], wu_[:, k, g * P:(g + 1) * P], hT[:, k, cols], k == 0, k == KC - 1)
                tt(ybr[:, g, cols], pu[:, 0:n], sall[:, g, cols], ALU.mult)
        gate_merge(l, grp, 2, wbo_gmlp, 4)

        ada_drain(l, 1)
        wos = []
        for s4 in range(2):
            (wo,) = wload([(0, [KC, 512], wcols(w_out[l], s4 * 512, 512))])
            wos.append(wo)

        def body(t, m):
            n = t.n
            wo = wos[m // 4]
            m4 = m % 4
            py = ps("mm")
            for k in range(KC):
                mm(py[:, 0:n], wo[:, k, m4 * P:(m4 + 1) * P], merged[:, k, t.col:t.col + n], k == 0, k == KC - 1)
            resid_update(l, 5, m, t, py)
        tile_outer(grp, body, after1)

    def final_stage1(t):
        s2 = rstd_stage1(t)

        def stage2():
            yT = scrv(5120, [KC, 512])
            n = t.n
            rstd = s2()
            for k in range(KC):
                stt(yT[:, k, 0:n], xT[:, k, t.col:t.col + n], vecT[:, V_NF + k:V_NF + k + 1], rstd, ALU.mult, ALU.mult)
            for sub in range(n // P):
                for kb in range(2):
                    pt = ps("tr")
                    for k4 in range(4):
                        tr(pt[:, k4 * P:(k4 + 1) * P], yT[:, kb * 4 + k4, sub * P:(sub + 1) * P])
                    act(stg[:, kb * 512:(kb + 1) * 512], pt[:, :], AF.Copy)
                if t.kind == "p":
                    r0 = t.pos0 + sub * P
                    dma(y_p[r0:r0 + P, :], stg[:, :])
                else:
                    dma(y_s, stg[:, :])
        return stage2

    for gi, grp in enumerate(GROUPS):
        load_inputs(grp)
        if gi == 0:
            for fn in ada_up:
                fn()
        ada_ctl["grpA"] = (gi == 0)
        if gi == 1:
            build_bd()
        norm_mod(0, 0, grp)
        for l in range(NL):
            ada_ctl["lmax"] = l + 1
            ffn(l, 0, grp, None, after1=lambda t, l=l: norm_stage1(l, 1, t))
            mixer(l, grp, gi, after1=lambda t, l=l: norm_stage1(l, 2, t))
            if l + 1 < NL:
                ffn(l, 1, grp, None, after1=lambda t, l=l: norm_stage1(l + 1, 0, t))
            else:
                ffn(l, 1, grp, None, after1=final_stage1)
        assert not ada_q

    S.finalize(nc, stack)
    with nc.Block() as block:
        @block.tensor
        def _(e):
            S.emit("pe", e)

        @block.scalar
        def _(e):
            S.emit("act", e)

        @block.vector
        def _(e):
            S.emit("dve", e)

        @block.gpsimd
        def _(e):
            S.emit("pool", e)

        @block.sync
        def _(e):
            S.emit("sp", e)
    stack.close()
    return nc


_WNAMES = ["w_ada", "b_ada", "norm_ffn1", "ffn1_w_gate", "ffn1_w_up", "ffn1_w_down", "norm_mix", "w_in",
           "pool_w", "pool_scale", "conv_w", "conv_b", "lru_wr", "lru_br", "lru_wi", "lru_bi", "lru_lambda",
           "gmlp_norm", "gmlp_ws", "gmlp_bs", "wbo_pool", "wbo_lru", "wbo_gmlp", "w_out", "norm_ffn2",
           "ffn2_w_gate", "ffn2_w_up", "ffn2_w_down"]


def kernel(**inp):
    f = lambda a: np.ascontiguousarray(np.asarray(a, dtype=np.float32))
    nc = build_nc()
    shared = {n: f(inp[n]) for n in _WNAMES}
    shared["norm_final"] = f(inp["norm_final"]).reshape(1, D)
    xp = f(inp["x_prompt"]); xs = f(inp["x_sample"])
    cp_ = f(inp["c_prompt"]); cs = f(inp["c_sample"])
    sp = f(inp["state_pool"]); sc = f(inp["state_conv"]); sl = f(inp["state_lru"])
    in_maps = []
    for i in range(NCORES):
        sq = slice(i * NSEQ, (i + 1) * NSEQ)
        m = dict(shared)
        m["xp"] = xp[i]
        m["xs"] = xs[sq].reshape(NSEQ * TSAMP, D)
        m["c17"] = np.concatenate([cp_[i:i + 1], cs[sq]], axis=0)
        m["spool"] = sp[:, sq].reshape(NL, NSEQ * 15, D)
        m["sconv"] = sc[:, sq].reshape(NL, NSEQ * 3, D)
        m["slru"] = sl[:, sq].reshape(NL, NSEQ, D)
        in_maps.append(m)
    res = run_bass_kernel_spmd(nc, in_maps, core_ids=list(range(NCORES)))
    R = res.results
    y_prompt = np.stack([R[i]["y_p"] for i in range(NCORES)], axis=0)
    y_sample = np.concatenate([R[i]["y_s"].reshape(NSEQ, TSAMP, D) for i in range(NCORES)], axis=0)
    npp = np.stack([R[i]["o_pool_p"] for i in range(NCORES)], axis=1)
    ncp = np.stack([R[i]["o_conv_p"] for i in range(NCORES)], axis=1)
    nlp = np.stack([R[i]["o_lru_p"].reshape(NL, D) for i in range(NCORES)], axis=1)
    nps = np.concatenate([R[i]["o_pool_s"].reshape(NL, NSEQ, 15, D) for i in range(NCORES)], axis=1)
    ncs = np.concatenate([R[i]["o_conv_s"].reshape(NL, NSEQ, 3, D) for i in range(NCORES)], axis=1)
    nls = np.concatenate([R[i]["o_lru_s"] for i in range(NCORES)], axis=1)
    nvs = np.concatenate([R[i]["o_v_s"].reshape(NL, NSEQ, TSAMP, 512) for i in range(NCORES)], axis=1)
    return (y_prompt, y_sample, npp, ncp, nlp, nps, ncs, nls, nvs)
```
